# Optimizing a Trainium2 kernel written in Bass

```python
import jax, jax.numpy as jnp
from jax import lax
import numpy as np

D_MODEL = 1024
BATCH = 32
SEQ = 2048
DEPTH = 4

GRID_W = 64
CTX_LEN = 256
N_MIXERS = 3
NORM_EPS = 1e-6

A_WIDTH = 2 * D_MODEL
A_CHUNK = 128
A_GROUPS = 8
B_WIDTH = 2 * D_MODEL
B_WINDOWS = (2, 4, 8, 16)
B_GROUP_W = B_WIDTH // len(B_WINDOWS)
C_HEAD_DIM = 64
C_Q_HEADS = D_MODEL // C_HEAD_DIM
C_KV_HEADS = C_Q_HEADS // 4
C_GROUP = C_Q_HEADS // C_KV_HEADS
C_QW = C_Q_HEADS * C_HEAD_DIM
C_KVW = C_KV_HEADS * C_HEAD_DIM
C_WINDOW = 128
C_BLOCK = 128
ROPE_THETA = 10000.0

kernel_name = 'hybrid_gmlp_pool_swa_prefix_dit'


def _rmsnorm(x, g):
    xf = x.astype(jnp.float32)
    y = xf * lax.rsqrt(jnp.mean(xf * xf, axis=-1, keepdims=True) + NORM_EPS)
    return (y * g.astype(jnp.float32)).astype(x.dtype)


def _layernorm(x, g, b):
    xf = x.astype(jnp.float32)
    mu = jnp.mean(xf, axis=-1, keepdims=True)
    var = jnp.mean(jnp.square(xf - mu), axis=-1, keepdims=True)
    y = (xf - mu) * lax.rsqrt(var + NORM_EPS) * g.astype(jnp.float32) + b.astype(jnp.float32)
    return y.astype(x.dtype)


def _chunk_gmlp(h, w_in, vn_g, vn_b, w_s, b_s, w_out):
    bn, L, _ = h.shape
    u, v, gate = jnp.split(h @ w_in, 3, axis=-1)
    u = jax.nn.gelu(u)
    v = _layernorm(jax.nn.gelu(v), vn_g, vn_b)
    vc = v.reshape(bn, L // A_CHUNK, A_CHUNK, A_GROUPS, A_WIDTH // A_GROUPS)
    s = jnp.einsum('gpq,bnqgc->bnpgc', w_s, vc) + b_s.T[None, None, :, :, None]
    y = u * s.reshape(bn, L, A_WIDTH)
    return (y * jax.nn.silu(gate)) @ w_out


def _centred_mean(p, w):
    bn, L, ch = p.shape
    cs = jnp.concatenate([jnp.zeros((bn, 1, ch), jnp.float32),
                          jnp.cumsum(p.astype(jnp.float32), axis=1)], axis=1)
    t = jnp.arange(L)
    lo = jnp.clip(t - w // 2, 0, L)
    hi = jnp.clip(t + w // 2, 0, L)
    cnt = (hi - lo).astype(jnp.float32)
    return ((cs[:, hi] - cs[:, lo]) / cnt[None, :, None]).astype(p.dtype)


def _multiscale_pool(h, w_in, w_grp, scale, w_out):
    p, gate = jnp.split(h @ w_in, 2, axis=-1)
    groups = jnp.split(p, len(B_WINDOWS), axis=-1)
    pooled = jnp.stack([_centred_mean(g, w) - g for g, w in zip(groups, B_WINDOWS)], axis=2)
    mixed = jnp.einsum('blgc,gcd->blgd', pooled, w_grp).reshape(h.shape[0], h.shape[1], B_WIDTH) * scale
    return (mixed * jax.nn.silu(gate)) @ w_out


def _axial_rope_tables(rows):
    row = jnp.repeat(jnp.arange(rows), GRID_W).astype(jnp.float32)
    col = jnp.tile(jnp.arange(GRID_W), rows).astype(jnp.float32)
    n_freq = C_HEAD_DIM // 4
    inv = ROPE_THETA ** (-jnp.arange(n_freq, dtype=jnp.float32) / n_freq)
    ang = jnp.concatenate([row[:, None] * inv, col[:, None] * inv], axis=-1)
    return jnp.cos(ang), jnp.sin(ang)


def _apply_rope(x, cos, sin):
    shp = (1, cos.shape[0]) + (1,) * (x.ndim - 3) + (cos.shape[1],)
    c = cos.reshape(shp)
    s = sin.reshape(shp)
    x1, x2 = jnp.split(x.astype(jnp.float32), 2, axis=-1)
    return jnp.concatenate([x1 * c - x2 * s, x2 * c + x1 * s], axis=-1).astype(x.dtype)


def _heads_q(t):
    return t.reshape(t.shape[:2] + (C_KV_HEADS, C_GROUP, C_HEAD_DIM))


def _heads_kv(t):
    return t.reshape(t.shape[:2] + (C_KV_HEADS, C_HEAD_DIM))


def _banded_attention(q, k, v, kc, vc, sink):
    bn, S = q.shape[:2]
    lc = kc.shape[1]
    nb = S // C_BLOCK
    span = C_BLOCK + 2 * C_WINDOW
    pad = ((0, 0), (C_WINDOW, C_WINDOW), (0, 0), (0, 0))
    kp = jnp.pad(k, pad)
    vp = jnp.pad(v, pad)
    scale = C_HEAD_DIM ** -0.5
    rel = jnp.arange(C_BLOCK)[:, None] + C_WINDOW - jnp.arange(span)[None, :]
    band = jnp.abs(rel) <= C_WINDOW
    sink_b = sink.astype(jnp.float32).reshape(C_KV_HEADS, C_GROUP)[None, :, :, None, None]
    neg = jnp.finfo(jnp.float32).min

    def one_block(n):
        start = n * C_BLOCK
        qb = lax.dynamic_slice_in_dim(q, start, C_BLOCK, axis=1)
        kb = lax.dynamic_slice_in_dim(kp, start, span, axis=1)
        vb = lax.dynamic_slice_in_dim(vp, start, span, axis=1)
        kpos = start - C_WINDOW + jnp.arange(span)
        valid = band & ((kpos >= 0) & (kpos < S))[None, :]
        s_loc = jnp.einsum('bqkgd,bskd->bkgqs', qb, kb).astype(jnp.float32) * scale
        s_loc = jnp.where(valid, s_loc, neg)
        s_ctx = jnp.einsum('bqkgd,bskd->bkgqs', qb, kc).astype(jnp.float32) * scale
        s_snk = jnp.broadcast_to(sink_b, s_loc.shape[:-1] + (1,))
        p = jax.nn.softmax(jnp.concatenate([s_loc, s_ctx, s_snk], axis=-1), axis=-1)
        p_loc = p[..., :span].astype(v.dtype)
        p_ctx = p[..., span:span + lc].astype(v.dtype)
        return (jnp.einsum('bkgqs,bskd->bqkgd', p_loc, vb)
                + jnp.einsum('bkgqs,bskd->bqkgd', p_ctx, vc))

    outs = lax.map(one_block, jnp.arange(nb))
    return jnp.moveaxis(outs, 0, 1).reshape(bn, S, C_QW)


def _context_attention(qc, kc, vc, sink):
    scale = C_HEAD_DIM ** -0.5
    s = jnp.einsum('bqkgd,bskd->bkgqs', qc, kc).astype(jnp.float32) * scale
    sink_b = sink.astype(jnp.float32).reshape(C_KV_HEADS, C_GROUP)[None, :, :, None, None]
    s = jnp.concatenate([s, jnp.broadcast_to(sink_b, s.shape[:-1] + (1,))], axis=-1)
    p = jax.nn.softmax(s, axis=-1)[..., :-1].astype(vc.dtype)
    o = jnp.einsum('bkgqs,bskd->bqkgd', p, vc)
    return o.reshape(qc.shape[0], qc.shape[1], C_QW)


def _n_of_kind(kind):
    return len(range(kind, DEPTH, N_MIXERS))


def setup_inputs(seed: int = 0) -> dict:
    key = jax.random.key(seed)
    ks = iter(jax.random.split(key, 32))
    nrm = lambda shape, s: jax.random.normal(next(ks), shape, jnp.float32) * s
    nA, nB, nC = _n_of_kind(0), _n_of_kind(1), _n_of_kind(2)
    D = D_MODEL
    return {
        'x': nrm((BATCH, SEQ, D), 1.0),
        'c': nrm((BATCH, D), 1.0),
        'ctx': nrm((BATCH, CTX_LEN, D), 1.0),
        'c_ctx': nrm((D,), 1.0),
        'ada_w': nrm((DEPTH, D, 3 * D), 0.5 * D ** -0.5),
        'ada_b': nrm((DEPTH, 3 * D), 0.02),
        'norm_g': 1.0 + nrm((DEPTH, D), 0.02),
        'final_g': 1.0 + nrm((D,), 0.02),
        'gmlp_w_in': nrm((nA, D, 3 * A_WIDTH), D ** -0.5),
        'gmlp_vnorm_g': 1.0 + nrm((nA, A_WIDTH), 0.02),
        'gmlp_vnorm_b': nrm((nA, A_WIDTH), 0.02),
        'gmlp_w_s': nrm((nA, A_GROUPS, A_CHUNK, A_CHUNK), A_CHUNK ** -0.5),
        'gmlp_b_s': 1.0 + nrm((nA, A_GROUPS, A_CHUNK), 0.02),
        'gmlp_w_out': nrm((nA, A_WIDTH, D), A_WIDTH ** -0.5),
        'pool_w_in': nrm((nB, D, 2 * B_WIDTH), D ** -0.5),
        'pool_w_grp': nrm((nB, len(B_WINDOWS), B_GROUP_W, B_GROUP_W), B_GROUP_W ** -0.5),
        'pool_scale': 1.0 + nrm((nB, B_WIDTH), 0.1),
        'pool_w_out': nrm((nB, B_WIDTH, D), B_WIDTH ** -0.5),
        'attn_w_in': nrm((nC, D, 2 * C_QW + 2 * C_KVW), D ** -0.5),
        'attn_sink': nrm((nC, C_Q_HEADS), 1.0),
        'attn_w_out': nrm((nC, C_QW, D), C_QW ** -0.5),
    }


def reference(x, c, ctx, c_ctx, ada_w, ada_b, norm_g, final_g,
              gmlp_w_in, gmlp_vnorm_g, gmlp_vnorm_b, gmlp_w_s, gmlp_b_s, gmlp_w_out,
              pool_w_in, pool_w_grp, pool_scale, pool_w_out,
              attn_w_in, attn_sink, attn_w_out):
    ROWS = x.shape[1] // GRID_W
    cos, sin = _axial_rope_tables(ROWS)
    cond_lat = jax.nn.silu(c.astype(jnp.float32))
    cond_ctx = jax.nn.silu(c_ctx.astype(jnp.float32))
    for i in range(DEPTH):
        kind = i % N_MIXERS
        j = i // N_MIXERS
        ctx_out = any(l % N_MIXERS == 2 for l in range(i + 1, DEPTH))
        ctx_in = ctx_out or kind == 2

        shift, scale, gate = jnp.split((cond_lat @ ada_w[i] + ada_b[i]).astype(x.dtype), 3, axis=-1)
        h = _rmsnorm(x, norm_g[i]) * (1 + scale[:, None]) + shift[:, None]
        if ctx_in:
            shift_c, scale_c, gate_c = jnp.split((cond_ctx @ ada_w[i] + ada_b[i]).astype(ctx.dtype), 3, axis=-1)
            hc = _rmsnorm(ctx, norm_g[i]) * (1 + scale_c) + shift_c

        if kind == 0:
            args = (gmlp_w_in[j], gmlp_vnorm_g[j], gmlp_vnorm_b[j], gmlp_w_s[j], gmlp_b_s[j], gmlp_w_out[j])
            y = _chunk_gmlp(h, *args)
            if ctx_out:
                yc = _chunk_gmlp(hc, *args)
        elif kind == 1:
            args = (pool_w_in[j], pool_w_grp[j], pool_scale[j], pool_w_out[j])
            y = _multiscale_pool(h, *args)
            if ctx_out:
                yc = _multiscale_pool(hc, *args)
        else:
            w_in = attn_w_in[j]
            q, k, v, g_att = jnp.split(h @ w_in, [C_QW, C_QW + C_KVW, C_QW + 2 * C_KVW], axis=-1)
            q = _apply_rope(_heads_q(q), cos, sin)
            k = _apply_rope(_heads_kv(k), cos, sin)
            v = _heads_kv(v)
            kc, vc = jnp.split(hc @ w_in[:, C_QW:C_QW + 2 * C_KVW], 2, axis=-1)
            kc = _heads_kv(kc)
            vc = _heads_kv(vc)
            o = _banded_attention(q, k, v, kc, vc, attn_sink[j])
            y = (o * jax.nn.silu(g_att)) @ attn_w_out[j]
            if ctx_out:
                qc = _heads_q(hc @ w_in[:, :C_QW])
                gc = hc @ w_in[:, C_QW + 2 * C_KVW:]
                yc = (_context_attention(qc, kc, vc, attn_sink[j]) * jax.nn.silu(gc)) @ attn_w_out[j]

        x = x + gate[:, None] * y
        if ctx_out:
            ctx = ctx + gate_c * yc
    return _rmsnorm(x, final_g)
```

```python
import numpy as np
from contextlib import ExitStack
import concourse.bass as bass
import concourse.mybir as mybir
from concourse.bass_utils import run_bass_kernel_spmd

F32 = mybir.dt.float32
BF16 = mybir.dt.bfloat16
AF = mybir.ActivationFunctionType
ALU = mybir.AluOpType

S = 2048
LC = 256
D = 1024
NLAYER = 4
EPS = 1e-6
NBLK = 55
BLK_BASE = {0: 0, 1: 16, 2: 32, 3: 39}
POOL_W = (2, 4, 8, 16)


class Eng:
    def __init__(self, name, h, sem, is_pe=False):
        self.name, self.h, self.sem, self.is_pe = name, h, sem, is_pe
        self.cnt = 0
        self.waited = {}


class DSem:
    def __init__(self, h):
        self.h = h
        self.cnt = 0


class Buf:
    __slots__ = ("w", "r")

    def __init__(self):
        self.w = None
        self.r = {}


class Sched:
    def __init__(self, nc, es):
        self.nc, self.es = nc, es
        self.nsem = 0
        self.pe = Eng("pe", nc.tensor, self.sem("s_pe"), True)
        self.act = Eng("act", nc.scalar, self.sem("s_act"))
        self.dve = Eng("dve", nc.vector, self.sem("s_dve"))
        self.pool = Eng("pool", nc.gpsimd, self.sem("s_pool"))
        self.sp = Eng("sp", nc.sync, self.sem("s_sp"))
        self.nins = 0

    def sem(self, name):
        self.nsem += 1
        return self.es.enter_context(self.nc.semaphore(name))

    def dsem(self, name):
        return DSem(self.sem(name))

    def _deps(self, eng, reads, writes):
        for b in reads:
            if b.w is not None:
                self._wait(eng, b.w, True)
        for b in writes:
            if b.w is not None:
                self._wait(eng, b.w, False)
            for t in b.r.values():
                self._wait(eng, t, False)

    def _wait(self, eng, tok, raw):
        sem, val, src = tok
        if src is eng and (eng.is_pe or not raw):
            return
        key = id(sem)
        if eng.waited.get(key, 0) >= val:
            return
        eng.h.wait_ge(sem, val)
        eng.waited[key] = val
        self.nins += 1

    @staticmethod
    def _commit(tok, reads, writes):
        for b in writes:
            b.w = tok
            b.r = {}
        k = id(tok[0])
        for b in reads:
            o = b.r.get(k)
            if o is None or o[1] < tok[1]:
                b.r[k] = tok

    def op(self, eng, fn, reads=(), writes=(), signal=True):
        self._deps(eng, reads, writes)
        ins = fn(eng.h)
        self.nins += 1
        if signal:
            eng.cnt += 1
            ins.then_inc(eng.sem, 1)
            tok = (eng.sem, eng.cnt, eng)
        else:
            tok = (eng.sem, eng.cnt + 1, eng)
        self._commit(tok, reads, writes)

    def dma(self, q, out, in_, dsem, reads=(), writes=()):
        self.dma_group(q, [(out, in_)], dsem, reads, writes)

    def dma_group(self, q, pairs, dsem, reads=(), writes=()):
        self._deps(q, reads, writes)
        for (o, i) in pairs:
            q.h.dma_start(out=o, in_=i).then_inc(dsem.h, 16)
            dsem.cnt += 16
            self.nins += 1
        tok = (dsem.h, dsem.cnt, None)
        self._commit(tok, reads, writes)

    def barrier(self):
        engs = (self.pe, self.act, self.dve, self.pool)
        for d in engs + (self.sp,):
            for e in engs:
                if e is not d and e.cnt > 0:
                    self._wait(d, (e.sem, e.cnt, e), True)

    def mm(self, bank, out_ap, pairs, reads):
        n = len(pairs)
        for k, (l, r) in enumerate(pairs):
            self.op(self.pe,
                    lambda h, l=l, r=r, k=k: h.matmul(out_ap, l, r, start=(k == 0), stop=(k == n - 1)),
                    reads=reads if k == 0 else (), writes=[bank] if k == 0 else (), signal=(k == n - 1))


class Tl:
    def __init__(self, res, rbuf, c0, n, ci, rs0):
        self.res, self.rbuf, self.c0, self.n, self.ci, self.rs0 = res, rbuf, c0, n, ci, rs0

    def x(self, dc, a=0, n=None):
        n = self.n if n is None else n
        return self.res[:, dc, self.c0 + a:self.c0 + a + n]


def build_program(NB=4, layers=(0, 1, 2, 3)):
    nc = bass.Bass("TRN2", target_bir_lowering=False)
    es = ExitStack()
    K = Sched(nc, es)
    NC = NB + 1

    def din(name, shape, dt=F32):
        return nc.dram_tensor(name, list(shape), dt, kind="ExternalInput").ap()

    xT = din("xT", [NB, 8, 128, S])
    cT = din("cT", [NB, 8, 128, LC])
    condT = din("condT", [128, 8, NC])
    adaw = din("adaw", [NLAYER, 24, 128, 8, 128])
    adab = din("adab", [128, NLAYER, 24])
    ngd = din("ng", [128, NLAYER, 8])
    fgd = din("fg", [128, 8])
    wts = din("wts", [NBLK, 128, 4096])
    g_wsT = din("g_wsT", [2, 128, 8, 128])
    g_bsbc = din("g_bsbc", [2, 128, 8, 128])
    g_vng = din("g_vng", [2, 128, 16])
    g_vnb = din("g_vnb", [2, 128, 16])
    p_scale = din("p_scale", [128, 16])
    p_icnt = din("p_icnt", [128, 4, 2, 8])
    a_sink = din("a_sink", [128, 8])
    rope_c = din("rope_c", [128, S])
    rope_s = din("rope_s", [128, S])
    cstf_d = din("cstf", [128, 1152])
    outT = nc.dram_tensor("outT", [NB, 8, 128, S], F32, kind="ExternalOutput").ap()
    wtb = nc.dram_tensor("wtb", [NBLK, 128, 4096], BF16).ap()
    import os as _os
    DBG = _os.environ.get("KDBG") == "1"
    if DBG:
        dbgf = nc.dram_tensor("dbgf", [128, 4096], F32, kind="ExternalOutput").ap()
        dbgb = nc.dram_tensor("dbgb", [128, 4096], BF16, kind="ExternalOutput").ap()
        dbs = K.dsem("dbs")

    _uid = [0]

    def sb(st, name, shape, dt):
        _uid[0] += 1
        return st.enter_context(nc.sbuf_tensor("%s_%d" % (name, _uid[0]), list(shape), dt))

    PS = [es.enter_context(nc.psum_tensor(f"ps{k}", [128, 512], F32)) for k in range(8)]
    PSB = [Buf() for _ in range(8)]

    class BankRing:
        def __init__(self):
            self.ids = list(range(8))
            self.i = 0

        def set(self, ids):
            self.ids = list(ids)
            self.i = 0

        def next(self):
            b = self.ids[self.i % len(self.ids)]
            self.i += 1
            return b

    banks = BankRing()

    eps_t = sb(es, "eps_t", [128, 1], F32)
    ones_mean = sb(es, "ones_mean", [128, 128], BF16)
    ones_f = sb(es, "ones_f", [128, 128], F32)
    OZ = sb(es, "OZ", [128, 192], BF16)
    cstb = sb(es, "cstb", [128, 1152], BF16)
    adab_t = sb(es, "adab_t", [128, NLAYER, 24], F32)
    ng_t = sb(es, "ng_t", [128, NLAYER, 8], F32)
    fg_t = sb(es, "fg_t", [128, 8], F32)
    mod = sb(es, "mod", [128, NLAYER, 24, NC], F32)
    GF = sb(es, "GF", [128, NLAYER, NC, 8], F32)
    CB = Buf()
    MODB = Buf()
    GFB = Buf()

    NSLOT = 4
    ring = [sb(es, f"wr{k}", [128, 4096], BF16) for k in range(NSLOT)]
    ringb = [Buf() for _ in range(NSLOT)]
    rings = [K.dsem(f"wrs{k}") for k in range(NSLOT)]
    convb = {bi: Buf() for bi in range(NBLK)}
    convs = {bi: K.dsem(f"cv{bi}") for bi in range(NBLK)}

    def blk_layer(bi):
        for l in (3, 2, 1, 0):
            if bi >= BLK_BASE[l]:
                return l

    wseq = []
    wstate = {"issued": 0, "used": 0}

    def plan_weights():
        for b in range(NB):
            for l in layers:
                base = BLK_BASE[l]
                if l in (0, 3):
                    ntile = 5 if l == 0 else 4
                    for _ in range(ntile):
                        wseq.extend([base + 4 + c for c in range(4)])
                        wseq.extend([base + c for c in range(4)])
                        wseq.extend([base + 8 + c for c in range(4)])
                        wseq.extend([base + 12 + c for c in range(4)])
                elif l == 1:
                    for hf in range(2):
                        for g in range(4):
                            wseq.extend([base + g, base + 4 + g, base + 8 + g, base + 12 + g])
                else:
                    for _ in range(5):
                        wseq.append(base + 2)
                    for _ in range(4):
                        wseq.extend([base + 0, base + 1, base + 3, base + 4, base + 5, base + 6])

    def acquire(bi):
        i = wstate["used"]
        assert wseq[i] == bi, (i, wseq[i], bi)
        wstate["used"] += 1
        while wstate["issued"] < min(len(wseq), i + NSLOT - 1):
            k = wstate["issued"]
            s = k % NSLOT
            K.dma(K.sp, ring[s][:], wtb[wseq[k]], rings[s], reads=[convb[wseq[k]]], writes=[ringb[s]])
            wstate["issued"] += 1
        s = i % NSLOT
        return ring[s], ringb[s]

    X = sb(es, "X", [128, 8, S], F32)
    C = sb(es, "C", [128, 8, LC], F32)
    rstd = sb(es, "rstd", [128, S + LC], F32)
    XB = [Buf() for _ in range(4)]
    CBUF = Buf()
    RB = [Buf() for _ in range(5)]
    xs = K.dsem("xld")
    xs_t = [K.dsem("xld%d" % t) for t in range(4)]
    os_t = [K.dsem("ost%d" % t) for t in range(4)]
    lcs = [K.dsem("lc0"), K.dsem("lc1")]
    lcn = [0]

    def lc_sem():
        lcn[0] += 1
        return lcs[lcn[0] % 2]

    lat_tiles = [Tl(X, XB[t], 512 * t, 512, None, 512 * t) for t in range(4)]
    ctx_tile = Tl(C, CBUF, 0, LC, NB, S)

    def load_inputs(b):
        for t in range(4):
            K.dma(K.sp, X[:, :, 512 * t:512 * t + 512], xT[b][:, :, 512 * t:512 * t + 512].rearrange("c p t -> p c t"),
                  xs_t[t], writes=[XB[t]])
        K.dma_group(K.sp, [(C[:], cT[b].rearrange("c p t -> p c t"))], xs, writes=[CBUF])

    load_inputs(0)

    def convert(l):
        base = BLK_BASE[l]
        if l in (0, 3):
            order = [4, 5, 6, 7, 0, 1, 2, 3] + list(range(8, 16))
        elif l == 1:
            order = [g + 4 * q for g in range(4) for q in range(4)]
        else:
            order = [2, 0, 1, 3, 4, 5, 6]
        for o in order:
            K.dma(K.pool, wtb[base + o], wts[base + o], convs[base + o], writes=[convb[base + o]])

    convert(layers[0]) if layers else None

    cs = K.dsem("cst")
    with ExitStack() as ps_:
        cstf = sb(ps_, "cstf_t", [128, 1152], F32)
        cond = sb(ps_, "cond", [128, 8, NC], F32)
        conds = sb(ps_, "conds", [128, 8, NC], F32)
        stage = [sb(ps_, f"adst{k}", [128, 4, 8, 128], F32) for k in range(2)]
        stageb = [Buf(), Buf()]
        stages = [K.dsem("adst_s0"), K.dsem("adst_s1")]
        tb = Buf()
        K.dma_group(K.sp, [(cstf[:], cstf_d), (cond[:], condT), (adab_t[:], adab), (ng_t[:], ngd), (fg_t[:], fgd)],
                    cs, writes=[tb])
        K.op(K.dve, lambda h: h.memset(eps_t[:], EPS), writes=[CB])
        K.op(K.dve, lambda h: h.memset(ones_mean[:], 1.0 / 1024.0), writes=[CB])
        K.op(K.dve, lambda h: h.memset(ones_f[:], 1.0), writes=[CB])
        K.op(K.dve, lambda h: h.memset(OZ[:], 1.0), writes=[CB])
        K.op(K.dve, lambda h: h.memset(OZ[:, 64:128], 0.0), writes=[CB])
        K.op(K.dve, lambda h: h.tensor_copy(out=cstb[:], in_=cstf[:]), reads=[tb], writes=[CB])
        condb = Buf()
        K.op(K.act, lambda h: h.activation(out=conds[:], in_=cond[:], func=AF.Silu), reads=[tb], writes=[condb])
        k = 0
        for l in layers:
            for t6 in range(6):
                s = k % 2
                k += 1
                K.dma(K.sp, stage[s][:], adaw[l, 4 * t6:4 * t6 + 4].rearrange("c p d m -> p c d m"), stages[s],
                      writes=[stageb[s]])
                for m in range(4):
                    cc = 4 * t6 + m
                    bk = banks.next()
                    K.mm(PSB[bk], PS[bk][:, 0:NC], [(stage[s][:, m, dc, :], conds[:, dc, :]) for dc in range(8)],
                         reads=[stageb[s], condb])
                    K.op(K.dve, lambda h, bk=bk, l=l, cc=cc: h.tensor_scalar(
                        out=mod[:, l, cc, :], in0=PS[bk][:, 0:NC], scalar1=adab_t[:, l, cc:cc + 1], scalar2=None,
                        op0=ALU.add), reads=[PSB[bk], tb], writes=[MODB])
        for l in layers:
            for n in range(NC):
                K.op(K.dve, lambda h, l=l, n=n: h.tensor_scalar(
                    out=GF[:, l, n, :], in0=mod[:, l, 8:16, n], scalar1=1.0, scalar2=None, op0=ALU.add),
                    reads=[MODB], writes=[GFB])
                K.op(K.dve, lambda h, l=l, n=n: h.tensor_tensor(
                    out=GF[:, l, n, :], in0=GF[:, l, n, :], in1=ng_t[:, l, :], op=ALU.mult),
                    reads=[GFB, tb], writes=[GFB])
        K.barrier()
    Pm = cstb[:, 0:128]
    maskP = cstb[:, 128:640]
    maskN = cstb[:, 640:1152]

    def scope_barrier(allb):
        K.barrier()

    def emit_rstd(tiles, sqs, sd, sdb):
        sqbs = [[Buf() for _ in range(8)] for _ in range(2)]

        def squares(t):
            tl = tiles[t]
            n = tl.n
            sq = sqs[t % 2]
            for dc in range(8):
                e = (K.act, K.dve, K.pool)[dc % 3] if dc < 6 else (K.act, K.dve)[dc % 2]
                if e is K.act:
                    K.op(e, lambda h, dc=dc: h.activation(out=sq(dc, n), in_=tl.x(dc), func=AF.Square),
                         reads=[tl.rbuf], writes=[sqbs[t % 2][dc]])
                else:
                    K.op(e, lambda h, dc=dc: h.tensor_tensor(out=sq(dc, n), in0=tl.x(dc), in1=tl.x(dc),
                                                             op=ALU.mult), reads=[tl.rbuf], writes=[sqbs[t % 2][dc]])

        squares(0)
        for t, tl in enumerate(tiles):
            n = tl.n
            sq = sqs[t % 2]
            if t + 1 < len(tiles):
                squares(t + 1)
            bk = banks.next()
            K.mm(PSB[bk], PS[bk][:, 0:n], [(ones_mean[:], sq(dc, n)) for dc in range(8)], reads=sqbs[t % 2] + [CB])
            K.op(K.act, lambda h, bk=bk, n=n: h.activation(out=sd[:, 0:n], in_=PS[bk][:, 0:n], func=AF.Sqrt,
                                                            bias=eps_t[:], scale=1.0),
                 reads=[PSB[bk], CB], writes=[sdb])
            ri = tl.rs0 // 512
            K.op(K.dve, lambda h, tl=tl, n=n: h.reciprocal(out=rstd[:, tl.rs0:tl.rs0 + n], in_=sd[:, 0:n]),
                 reads=[sdb], writes=[RB[ri]])
        K.barrier()

    def emit_h(tl, l, ci, hdst, hb, tscr, tscrb, a=0, n=None):
        n = tl.n if n is None else n
        ri = tl.rs0 // 512
        for dc in range(8):
            k = dc % len(tscr)
            K.op(K.dve, lambda h, dc=dc, k=k: h.tensor_tensor(
                out=tscr[k][:, 0:n], in0=tl.x(dc, a, n), in1=rstd[:, tl.rs0 + a:tl.rs0 + a + n], op=ALU.mult),
                reads=[tl.rbuf, RB[ri]], writes=[tscrb[k]])
            K.op(K.act, lambda h, dc=dc, k=k: h.activation(
                out=hdst(dc), in_=tscr[k][:, 0:n], func=AF.Identity, bias=mod[:, l, dc, ci:ci + 1],
                scale=GF[:, l, ci, dc:dc + 1]), reads=[tscrb[k], MODB, GFB], writes=[hb])

    def emit_resid(tl, l, ci, dmc, bk, a=0, n=None):
        n = tl.n if n is None else n
        K.op(K.dve, lambda h: h.scalar_tensor_tensor(
            out=tl.x(dmc, a, n), in0=PS[bk][:, 0:n], scalar=mod[:, l, 16 + dmc, ci:ci + 1], in1=tl.x(dmc, a, n),
            op0=ALU.mult, op1=ALU.add), reads=[PSB[bk], MODB], writes=[tl.rbuf])

    def emit_gmlp(l, j, b, tiles):
        base = BLK_BASE[l]
        with ExitStack() as st:
            allb = []

            def nb_():
                bb = Buf()
                allb.append(bb)
                return bb
            wsT_f = sb(st, "wsT_f", [128, 8, 128], F32)
            bsbc = sb(st, "bsbc", [128, 8, 128], F32)
            vng = sb(st, "vng", [128, 16], F32)
            vnb = sb(st, "vnb", [128, 16], F32)
            wsT_b = sb(st, "wsT_b", [128, 8, 128], BF16)
            Bmat = sb(st, "Bmat", [128, 16, 128], F32)
            hh = [sb(st, f"gh{k}", [128, 8, 512], BF16) for k in range(2)]
            hhb = [nb_(), nb_()]
            tscr = [sb(st, f"gts{k}", [128, 512], F32) for k in range(2)]
            tscrb = [nb_(), nb_()]
            gv = sb(st, "gv", [128, 4, 2048], BF16)
            gvb = [nb_() for _ in range(4)]
            u = sb(st, "gu", [128, 16, 512], BF16)
            ub = [nb_() for _ in range(16)]
            sgr = [sb(st, f"gsg{k}", [128, 512], BF16) for k in range(3)]
            sgb = [nb_() for _ in range(3)]
            t1 = [sb(st, f"gt1{k}", [128, 512], BF16) for k in range(2)]
            t1b = [nb_(), nb_()]
            stt = sb(st, "gstt", [128, 4, 24], F32)
            sttb = nb_()
            mv = sb(st, "gmv", [128, 4, 2], F32)
            mvb = nb_()
            sdv = sb(st, "gsdv", [128, 4], F32)
            rsv = sb(st, "grsv", [128, 4], F32)
            rsvb = nb_()
            sd = sb(st, "gsd", [128, 512], F32)
            sdb = nb_()
            lb = nb_()
            bmb = nb_()
            K.dma_group(K.sp, [(wsT_f[:], g_wsT[j]), (bsbc[:], g_bsbc[j]), (vng[:], g_vng[j]), (vnb[:], g_vnb[j])],
                        lc_sem(), writes=[lb])
            K.op(K.dve, lambda h: h.tensor_copy(out=wsT_b[:], in_=wsT_f[:]), reads=[lb], writes=[bmb])
            bk2 = []
            for hf in range(2):
                bk = banks.next()
                bk2.append(bk)
                K.mm(PSB[bk], PS[bk][:, :], [(ones_f[:], wsT_f[:, 4 * hf:4 * hf + 4, :])], reads=[lb, CB])
            for cc in range(16):
                g = cc // 2
                bk = bk2[g // 4]
                K.op(K.dve, lambda h, cc=cc, g=g, bk=bk: h.scalar_tensor_tensor(
                    out=Bmat[:, cc, :], in0=PS[bk][:, (g % 4) * 128:(g % 4) * 128 + 128], scalar=vnb[:, cc:cc + 1],
                    in1=bsbc[:, g, :], op0=ALU.mult, op1=ALU.add), reads=[PSB[bk], lb], writes=[bmb])
            emit_rstd(tiles, [lambda dc, n: u[:, dc, 0:n], lambda dc, n: u[:, 8 + dc, 0:n]], sd, sdb)

            def cidx(tl):
                return tl.ci if tl.ci is not None else b

            def hd(k, n):
                return lambda dc: hh[k][:, dc, 0:n]
            emit_h(tiles[0], l, cidx(tiles[0]), hd(0, tiles[0].n), hhb[0], tscr, tscrb)
            for ti, tl in enumerate(tiles):
                n = tl.n
                nj = n // 128
                ci = cidx(tl)
                h = hh[ti % 2]
                hb = hhb[ti % 2]
                for cg in range(4):
                    slot, slb = acquire(base + 4 + cg)
                    sv = slot[:].rearrange("p (m d k) -> p m d k", m=4, d=8)
                    for jj in range(nj):
                        bk = banks.next()
                        K.mm(PSB[bk], PS[bk][:, :],
                             [(h[:, dc, jj * 128:(jj + 1) * 128], sv[:, :, dc, :]) for dc in range(8)], reads=[hb, slb])
                        K.op(K.act, lambda hh_, bk=bk, jj=jj, cg=cg: hh_.activation(
                            out=gv[:, jj, cg * 512:(cg + 1) * 512], in_=PS[bk][:, :], func=AF.Gelu_apprx_tanh),
                            reads=[PSB[bk]], writes=[gvb[jj]])
                        K.op(K.dve, lambda hh_, jj=jj, cg=cg: hh_.bn_stats(
                            out=stt[:, jj, cg * 6:(cg + 1) * 6], in_=gv[:, jj, cg * 512:(cg + 1) * 512]),
                            reads=[gvb[jj]], writes=[sttb])
                for jj in range(nj):
                    K.op(K.dve, lambda hh_, jj=jj: hh_.bn_aggr(out=mv[:, jj, :], in_=stt[:, jj, :]),
                         reads=[sttb], writes=[mvb])
                K.op(K.act, lambda hh_: hh_.activation(out=sdv[:, 0:nj], in_=mv[:, 0:nj, 1], func=AF.Sqrt,
                                                       bias=eps_t[:], scale=1.0), reads=[mvb, CB], writes=[rsvb])
                K.op(K.dve, lambda hh_: hh_.reciprocal(out=rsv[:, 0:nj], in_=sdv[:, 0:nj]), reads=[rsvb], writes=[rsvb])
                for jj in range(nj):
                    K.op(K.dve, lambda hh_, jj=jj: hh_.tensor_scalar(
                        out=gv[:, jj, :], in0=gv[:, jj, :], scalar1=mv[:, jj, 0:1], scalar2=rsv[:, jj:jj + 1],
                        op0=ALU.subtract, op1=ALU.mult), reads=[gvb[jj], mvb, rsvb], writes=[gvb[jj]])
                for cg in range(4):
                    slot, slb = acquire(base + cg)
                    sv = slot[:].rearrange("p (m d k) -> p m d k", m=4, d=8)
                    for m in range(4):
                        cc = 4 * cg + m
                        bk = banks.next()
                        K.mm(PSB[bk], PS[bk][:, 0:n], [(sv[:, m, dc, :], h[:, dc, 0:n]) for dc in range(8)],
                             reads=[hb, slb])
                        K.op(K.act, lambda hh_, bk=bk, cc=cc: hh_.activation(
                            out=u[:, cc, 0:n], in_=PS[bk][:, 0:n], func=AF.Gelu_apprx_tanh),
                            reads=[PSB[bk]], writes=[ub[cc]])
                def spatial(cc):
                    g = cc // 2
                    bk = banks.next()
                    for jj in range(nj):
                        K.op(K.pe, lambda hh_, bk=bk, jj=jj, cc=cc, g=g: hh_.matmul(
                            PS[bk][:, jj * 128:(jj + 1) * 128], gv[:, jj, cc * 128:(cc + 1) * 128], wsT_b[:, g, :],
                            start=True, stop=True), reads=[gvb[jj], bmb], writes=[PSB[bk]], signal=(jj == nj - 1))
                    k2 = cc % 2
                    K.op(K.dve, lambda hh_, bk=bk, cc=cc, k2=k2: hh_.scalar_tensor_tensor(
                        out=t1[k2][:, 0:n].rearrange("p (j q) -> p j q", q=128),
                        in0=PS[bk][:, 0:n].rearrange("p (j q) -> p j q", q=128), scalar=vng[:, cc:cc + 1],
                        in1=Bmat[:, cc, :].unsqueeze(1).to_broadcast([128, nj, 128]), op0=ALU.mult, op1=ALU.add),
                        reads=[PSB[bk], lb, bmb], writes=[t1b[k2]])
                    K.op(K.dve, lambda hh_, cc=cc, k2=k2: hh_.tensor_tensor(
                        out=u[:, cc, 0:n], in0=t1[k2][:, 0:n], in1=u[:, cc, 0:n], op=ALU.mult),
                        reads=[t1b[k2], ub[cc]], writes=[ub[cc]])

                kk = 0
                for cg in range(4):
                    slot, slb = acquire(base + 8 + cg)
                    sv = slot[:].rearrange("p (m d k) -> p m d k", m=4, d=8)
                    for m in range(4):
                        cc = 4 * cg + m
                        bk = banks.next()
                        s3 = kk % 3
                        kk += 1
                        K.mm(PSB[bk], PS[bk][:, 0:n], [(sv[:, m, dc, :], h[:, dc, 0:n]) for dc in range(8)],
                             reads=[hb, slb])
                        K.op(K.act, lambda hh_, bk=bk, s3=s3: hh_.activation(
                            out=sgr[s3][:, 0:n], in_=PS[bk][:, 0:n], func=AF.Silu), reads=[PSB[bk]], writes=[sgb[s3]])
                        K.op(K.pool, lambda hh_, cc=cc, s3=s3: hh_.tensor_tensor(
                            out=u[:, cc, 0:n], in0=u[:, cc, 0:n], in1=sgr[s3][:, 0:n], op=ALU.mult),
                            reads=[ub[cc], sgb[s3]], writes=[ub[cc]])
                        if cc >= 2:
                            spatial(cc - 2)
                spatial(14)
                spatial(15)
                if ti + 1 < len(tiles):
                    nt = tiles[ti + 1]
                    emit_h(nt, l, cidx(nt), hd((ti + 1) % 2, nt.n), hhb[(ti + 1) % 2], tscr, tscrb)
                for dg in range(4):
                    slot, slb = acquire(base + 12 + dg)
                    so = slot[:].rearrange("p (m c k) -> p m c k", m=2, c=16)
                    for m in range(2):
                        dmc = 2 * dg + m
                        bk = banks.next()
                        for cc in range(16):
                            K.op(K.pe, lambda hh_, bk=bk, m=m, cc=cc: hh_.matmul(
                                PS[bk][:, 0:n], so[:, m, cc, :], u[:, cc, 0:n], start=(cc == 0), stop=(cc == 15)),
                                reads=[ub[cc], slb], writes=[PSB[bk]], signal=(cc == 15))
                        emit_resid(tl, l, ci, dmc, bk)
            scope_barrier(allb + ringb)

    def emit_pool(l, b):
        base = BLK_BASE[l]
        with ExitStack() as st:
            allb = []

            def nb_():
                bb = Buf()
                allb.append(bb)
                return bb
            HW = 1032 + LC
            hh = sb(st, "ph", [128, 8, HW], BF16)
            hbA, hb0, hb1, hb2 = nb_(), nb_(), nb_(), nb_()
            hhalo = sb(st, "phalo", [128, 8, 8], BF16)
            hhalob = nb_()
            tscr = [sb(st, f"pts{k}", [128, 512], F32) for k in range(2)]
            tscrb = [nb_(), nb_()]
            PL = [sb(st, f"pPL{k}", [128, 1048], F32) for k in range(2)]
            PLb = [nb_(), nb_()]
            PC = [sb(st, f"pPC{k}", [128, 272], F32) for k in range(2)]
            PCb = [nb_(), nb_()]
            T = [sb(st, f"pT{k}", [128, 1048], F32) for k in range(2)]
            Tb = [nb_(), nb_()]
            TC = [sb(st, f"pTC{k}", [128, 272], F32) for k in range(2)]
            TCb = [nb_(), nb_()]
            pooled = sb(st, "ppool", [128, 4, 1024 + LC], BF16)
            poolb = [nb_() for _ in range(4)]
            y = sb(st, "py", [128, 4, 1024 + LC], BF16)
            yb = [nb_() for _ in range(4)]
            sgp = sb(st, "psg", [128, 4, 1024 + LC], BF16)
            sgpb = [nb_() for _ in range(4)]
            psc = sb(st, "psc", [128, 16], F32)
            icnt = sb(st, "picnt", [128, 4, 2, 8], F32)
            bfix = sb(st, "pbfix", [128, 8], F32)
            bfixb = nb_()
            sq = sb(st, "psq", [128, 8, 512], BF16)
            sd = sb(st, "psd", [128, 512], F32)
            sdb = nb_()
            lb = nb_()
            K.dma_group(K.sp, [(psc[:], p_scale), (icnt[:], p_icnt)], lc_sem(), writes=[lb])
            for k in range(2):
                K.op(K.dve, lambda h, k=k: h.memset(PL[k][:], 0.0), writes=[PLb[k]])
                K.op(K.dve, lambda h, k=k: h.memset(PC[k][:], 0.0), writes=[PCb[k]])
            emit_rstd(lat_tiles + [ctx_tile], [lambda dc, n: sq[:, dc, 0:n],
                                                 lambda dc, n: pooled[:, dc // 2, (dc % 2) * 512:(dc % 2) * 512 + n]], sd, sdb)
            emit_h(lat_tiles[1], l, b, lambda dc: hhalo[:, dc, :], hhalob, tscr, tscrb, a=504, n=8)
            pk = 0
            for hf in range(2):
                t0_, t1_ = lat_tiles[2 * hf], lat_tiles[2 * hf + 1]
                emit_h(t0_, l, b, lambda dc: hh[:, dc, 8:520], hb0, tscr, tscrb)
                emit_h(t1_, l, b, lambda dc: hh[:, dc, 520:1032], hb1, tscr, tscrb)
                if hf == 0:
                    emit_h(lat_tiles[2], l, b, lambda dc: hh[:, dc, 0:8], hbA, tscr, tscrb, a=0, n=8)
                    pr = [(lambda dc: hh[:, dc, 8:520], 512, 8, False, [hb0]),
                          (lambda dc: hh[:, dc, 520:1032], 512, 520, False, [hb1]),
                          (lambda dc: hh[:, dc, 0:8], 8, 1032, False, [hbA])]
                    orr = [(lambda dc: hh[:, dc, 8:520], 512, 0, t0_, b, [hb0]),
                           (lambda dc: hh[:, dc, 520:1032], 512, 512, t1_, b, [hb1])]
                    io0 = 8
                else:
                    emit_h(ctx_tile, l, NB, lambda dc: hh[:, dc, 1032:1032 + LC], hb2, tscr, tscrb)
                    pr = [(lambda dc: hhalo[:, dc, :], 8, 8, False, [hhalob]),
                          (lambda dc: hh[:, dc, 8:520], 512, 16, False, [hb0]),
                          (lambda dc: hh[:, dc, 520:1032], 512, 528, False, [hb1]),
                          (lambda dc: hh[:, dc, 1032:1032 + LC], LC, 8, True, [hb2])]
                    orr = [(lambda dc: hh[:, dc, 8:520], 512, 0, t0_, b, [hb0]),
                           (lambda dc: hh[:, dc, 520:1032], 512, 512, t1_, b, [hb1]),
                           (lambda dc: hh[:, dc, 1032:1032 + LC], LC, 1024, ctx_tile, NB, [hb2])]
                    io0 = 16
                for g in range(4):
                    w = POOL_W[g]
                    hw_ = w // 2
                    nsteps = g + 1
                    slot, slb = acquire(base + g)
                    sv = slot[:].rearrange("p (m d k) -> p m d k", m=4, d=8)
                    for m in range(4):
                        k = pk % 2
                        pk += 1
                        for (rf, n, di, isc, rb) in pr:
                            bk = banks.next()
                            K.mm(PSB[bk], PS[bk][:, 0:n], [(sv[:, m, dc, :], rf(dc)) for dc in range(8)],
                                 reads=rb + [slb])
                            dst = PC[k] if isc else PL[k]
                            dstb = PCb[k] if isc else PLb[k]
                            K.op(K.act, lambda h, bk=bk, n=n, di=di, dst=dst: h.copy(
                                out=dst[:, di:di + n], in_=PS[bk][:, 0:n]), reads=[PSB[bk]], writes=[dstb])
                        todo = [(PL[k], PLb[k], T, Tb, 1048, io0, 1024, 0, hf == 0, hf == 1)]
                        if hf == 1:
                            todo.append((PC[k], PCb[k], TC, TCb, 272, 8, LC, 1024, True, True))
                        for (Pb, Pbb, TT, TTb, Ltot, io, no, oc, lfix, rfix) in todo:
                            src, srcb = Pb, Pbb
                            sh = 1
                            lo = 1
                            for stp in range(nsteps):
                                dstT, dstTb = TT[stp % 2], TTb[stp % 2]
                                K.op(K.pool, lambda h, src=src, dstT=dstT, sh=sh, lo=lo, Ltot=Ltot: h.tensor_tensor(
                                    out=dstT[:, lo:Ltot], in0=src[:, lo:Ltot], in1=src[:, lo - sh:Ltot - sh],
                                    op=ALU.add), reads=[srcb], writes=[dstTb])
                                src, srcb = dstT, dstTb
                                sh *= 2
                                lo = 2 * sh - 1
                            so_ = io + hw_ - 1
                            if DBG and hf == 0 and g == 1 and m == 0 and Ltot == 1048 and b == 0:
                                K.dma(K.sp, dbgf[:, 0:1048], Pb[:, :], dbs, reads=[Pbb])
                                K.dma(K.sp, dbgf[:, 1048:2096], src[:, :], dbs, reads=[srcb])
                            K.op(K.dve, lambda h, src=src, Pb=Pb, so_=so_, io=io, no=no, oc=oc, m=m, w=w:
                                 h.scalar_tensor_tensor(out=pooled[:, m, oc:oc + no], in0=src[:, so_:so_ + no],
                                                        scalar=1.0 / w, in1=Pb[:, io:io + no], op0=ALU.mult,
                                                        op1=ALU.subtract), reads=[srcb, Pbb], writes=[poolb[m]])
                            for (fix, side, c_) in ((lfix, 0, 0), (rfix, 1, no - 8)):
                                if not fix:
                                    continue
                                K.op(K.dve, lambda h, src=src, so_=so_, c_=c_, side=side, g=g: h.tensor_tensor(
                                    out=bfix[:], in0=src[:, so_ + c_:so_ + c_ + 8], in1=icnt[:, g, side, :],
                                    op=ALU.mult), reads=[srcb, lb], writes=[bfixb])
                                K.op(K.dve, lambda h, Pb=Pb, io=io, c_=c_, oc=oc, m=m: h.tensor_tensor(
                                    out=pooled[:, m, oc + c_:oc + c_ + 8], in0=bfix[:], in1=Pb[:, io + c_:io + c_ + 8],
                                    op=ALU.subtract), reads=[bfixb, Pbb], writes=[poolb[m]])
                    slotg, slgb = acquire(base + 4 + g)
                    svg = slotg[:].rearrange("p (m d k) -> p m d k", m=4, d=8)
                    slotw, slwb = acquire(base + 8 + g)
                    svw = slotw[:, 0:2048].rearrange("p (e c k) -> p e c k", e=4, c=4)
                    for dd in range(4):
                        for (rf, n, oc, tl, ci, rb) in orr:
                            bk = banks.next()
                            K.mm(PSB[bk], PS[bk][:, 0:n], [(svg[:, dd, dc, :], rf(dc)) for dc in range(8)],
                                 reads=rb + [slgb])
                            K.op(K.act, lambda h, bk=bk, n=n, oc=oc, dd=dd: h.activation(
                                out=sgp[:, dd, oc:oc + n], in_=PS[bk][:, 0:n], func=AF.Silu),
                                reads=[PSB[bk]], writes=[sgpb[dd]])
                    for dd in range(4):
                        for (rf, n, oc, tl, ci, rb) in orr:
                            bk = banks.next()
                            K.mm(PSB[bk], PS[bk][:, 0:n],
                                 [(svw[:, dd, cc, :], pooled[:, cc, oc:oc + n]) for cc in range(4)],
                                 reads=poolb + [slwb])
                            K.op(K.dve, lambda h, bk=bk, n=n, oc=oc, dd=dd, g=g: h.scalar_tensor_tensor(
                                out=y[:, dd, oc:oc + n], in0=PS[bk][:, 0:n], scalar=psc[:, 4 * g + dd:4 * g + dd + 1],
                                in1=sgp[:, dd, oc:oc + n], op0=ALU.mult, op1=ALU.mult),
                                reads=[PSB[bk], sgpb[dd], lb], writes=[yb[dd]])
                    if DBG and hf == 0 and g == 1 and b == 0:
                        K.dma(K.sp, dbgb[:, 0:1024], pooled[:, 0, 0:1024], dbs, reads=poolb)
                        K.dma(K.sp, dbgb[:, 1024:2048], y[:, 3, 0:1024], dbs, reads=yb)
                        K.dma(K.sp, dbgb[:, 2048:3072], sgp[:, 3, 0:1024], dbs, reads=sgpb)
                        K.dma(K.sp, dbgb[:, 3072:4096], hh[:, 0, 8:1032], dbs, reads=[hb0, hb1])
                    sloto, slob = acquire(base + 12 + g)
                    svo = sloto[:].rearrange("p (m e k) -> p m e k", m=8, e=4)
                    for dmc in range(8):
                        for (rf, n, oc, tl, ci, rb) in orr:
                            bk = banks.next()
                            K.mm(PSB[bk], PS[bk][:, 0:n], [(svo[:, dmc, dd, :], y[:, dd, oc:oc + n]) for dd in range(4)],
                                 reads=yb + [slob])
                            emit_resid(tl, l, ci, dmc, bk)
            scope_barrier(allb + ringb)

    def emit_attn(l, b):
        base = BLK_BASE[l]
        with ExitStack() as st:
            allb = []

            def nb_():
                bb = Buf()
                allb.append(bb)
                return bb
            kT = sb(st, "akT", [128, 2, 2, S + LC], BF16)
            kTb = [nb_() for _ in range(5)]
            VZ = sb(st, "aVZ", [128, 18, 2, 192], BF16)
            VZb = [nb_() for _ in range(5)]
            rc_t = sb(st, "arc", [128, 512], F32)
            rs_t = sb(st, "ars", [128, 512], F32)
            ropeb = nb_()
            rps = K.dsem("rps%d" % b)
            E = sb(st, "aE", [128, 8], F32)
            h = sb(st, "ah", [128, 8, 512], BF16)
            hb = nb_()
            tscr = [sb(st, f"ats{k}", [128, 512], F32) for k in range(2)]
            tscrb = [nb_(), nb_()]
            qT = sb(st, "aqT", [128, 8, 512], BF16)
            qTb = [nb_() for _ in range(8)]
            sg = sb(st, "asg", [128, 8, 512], BF16)
            sgb = [nb_() for _ in range(8)]
            y = sb(st, "ay", [128, 8, 512], BF16)
            yb = [nb_() for _ in range(8)]
            xb = [sb(st, f"axb{k}", [128, 512], BF16) for k in range(2)]
            xbb = [nb_(), nb_()]
            ta = [sb(st, f"ata{k}", [128, 512], F32) for k in range(2)]
            tab = [nb_(), nb_()]
            tb2 = [sb(st, f"atb{k}", [128, 512], F32) for k in range(2)]
            tbb = [nb_(), nb_()]
            NPT = 4
            PT = [sb(st, f"aPT{k}", [128, 512], BF16) for k in range(NPT)]
            PTb = [nb_() for _ in range(NPT)]
            dn = sb(st, "adn", [128, 512], F32)
            dnb = nb_()
            of = ta[0]
            ofb = tab[0]
            sd = dn
            sdb = nb_()
            lb = nb_()
            K.dma_group(K.sp, [(E[:], a_sink)], lc_sem(), writes=[lb])
            K.op(K.act, lambda hh_: hh_.activation(out=E[:], in_=E[:], func=AF.Exp), reads=[lb], writes=[lb])
            K.op(K.dve, lambda hh_: hh_.memset(VZ[:], 0.0), writes=VZb)
            K.op(K.pool, lambda hh_: hh_.memset(kT[:], 0.0), writes=kTb)
            tiles = lat_tiles + [ctx_tile]
            emit_rstd(tiles, [lambda dc, n: y[:, dc, 0:n], lambda dc, n: sg[:, dc, 0:n]], sd, sdb)
            rk = [0]

            def load_rope(tl):
                K.dma_group(K.sp, [(rc_t[:], rope_c[:, tl.c0:tl.c0 + 512]), (rs_t[:], rope_s[:, tl.c0:tl.c0 + 512])],
                            rps, writes=[ropeb])

            def rope(bk, n, c0, dst, dstb):
                k = rk[0] % 2
                rk[0] += 1
                actdone = Buf()
                K.op(K.act, lambda hh_: hh_.copy(out=xb[k][:, 0:n], in_=PS[bk][:, 0:n]), reads=[PSB[bk]],
                     writes=[xbb[k], actdone])
                bk2 = banks.next()
                K.mm(PSB[bk2], PS[bk2][:, 0:n], [(Pm, xb[k][:, 0:n])], reads=[xbb[k], CB])
                K.op(K.dve, lambda hh_: hh_.tensor_tensor(out=ta[k][:, 0:n], in0=PS[bk][:, 0:n],
                                                          in1=rc_t[:, 0:n], op=ALU.mult),
                     reads=[PSB[bk], ropeb, actdone], writes=[tab[k]])
                K.op(K.dve, lambda hh_: hh_.tensor_tensor(out=tb2[k][:, 0:n], in0=PS[bk2][:, 0:n],
                                                          in1=rs_t[:, 0:n], op=ALU.mult),
                     reads=[PSB[bk2], ropeb], writes=[tbb[k]])
                if isinstance(dst, list):
                    for (psl, d_) in dst:
                        K.op(K.pool, lambda hh_, psl=psl, d_=d_: hh_.tensor_tensor(
                            out=d_, in0=ta[k][psl, 0:n], in1=tb2[k][psl, 0:n], op=ALU.add),
                            reads=[tab[k], tbb[k]], writes=[dstb])
                else:
                    K.op(K.pool, lambda hh_: hh_.tensor_tensor(out=dst, in0=ta[k][:, 0:n], in1=tb2[k][:, 0:n],
                                                               op=ALU.add),
                         reads=[tab[k], tbb[k]], writes=[dstb])

            for ti, tl in enumerate(tiles):
                n = tl.n
                isc = tl is ctx_tile
                ci = NB if isc else b
                if not isc:
                    load_rope(tl)
                emit_h(tl, l, ci, lambda dc: h[:, dc, 0:n], hb, tscr, tscrb)
                slot, slb = acquire(base + 2)
                sv = slot[:].rearrange("p (m d k) -> p m d k", m=4, d=8)
                kc0 = tl.rs0
                for i in range(2):
                    bk = banks.next()
                    K.mm(PSB[bk], PS[bk][:, 0:n], [(sv[:, i, dc, :], h[:, dc, 0:n]) for dc in range(8)], reads=[hb, slb])
                    if isc:
                        K.op(K.act, lambda hh_, bk=bk, i=i: hh_.copy(out=kT[0:64, i, 0, kc0:kc0 + n],
                                                                     in_=PS[bk][0:64, 0:n]),
                             reads=[PSB[bk]], writes=[kTb[ti]])
                        K.op(K.act, lambda hh_, bk=bk, i=i: hh_.copy(out=kT[64:128, i, 1, kc0:kc0 + n],
                                                                     in_=PS[bk][64:128, 0:n]),
                             reads=[PSB[bk]], writes=[kTb[ti]])
                    else:
                        rope(bk, n, tl.c0, [(slice(0, 64), kT[0:64, i, 0, kc0:kc0 + n]),
                                            (slice(64, 128), kT[64:128, i, 1, kc0:kc0 + n])], kTb[ti])
                for jj in range(n // 128):
                    ch = kc0 // 128 + jj
                    bk = banks.next()
                    K.mm(PSB[bk], PS[bk][:, 0:256],
                         [(h[:, dc, jj * 128:(jj + 1) * 128], sv[:, 2:4, dc, :]) for dc in range(8)], reads=[hb, slb])
                    pv = PS[bk][:, 0:256].rearrange("p (i c) -> p i c", i=2)
                    K.op(K.act, lambda hh_, ch=ch, pv=pv: hh_.copy(out=VZ[:, ch, :, 0:64], in_=pv[:, :, 0:64]),
                         reads=[PSB[bk]], writes=[VZb[ti]])
                    K.op(K.act, lambda hh_, ch=ch, pv=pv: hh_.copy(out=VZ[:, ch, :, 128:192], in_=pv[:, :, 64:128]),
                         reads=[PSB[bk]], writes=[VZb[ti]])
            banks.set([4, 5, 6, 7])
            ASTG = int(_os.environ.get("ASTG", "9"))
            for ti, tl in enumerate(lat_tiles):
                n = 512
                if ti == 0:
                    load_rope(tl)
                    emit_h(tl, l, b, lambda dc: h[:, dc, :], hb, tscr, tscrb)
                for blk in range(2):
                    slot, slb = acquire(base + blk)
                    sv = slot[:].rearrange("p (m d k) -> p m d k", m=4, d=8)
                    for m in range(4):
                        qc = 4 * blk + m
                        bk = banks.next()
                        K.mm(PSB[bk], PS[bk][:, :], [(sv[:, m, dc, :], h[:, dc, :]) for dc in range(8)], reads=[hb, slb])
                        rope(bk, 512, tl.c0, qT[:, qc, :], qTb[qc])
                for blk in range(2):
                    slot, slb = acquire(base + 3 + blk)
                    sv = slot[:].rearrange("p (m d k) -> p m d k", m=4, d=8)
                    for m in range(4):
                        qc = 4 * blk + m
                        bk = banks.next()
                        K.mm(PSB[bk], PS[bk][:, :], [(sv[:, m, dc, :], h[:, dc, :]) for dc in range(8)], reads=[hb, slb])
                        K.op(K.act, lambda hh_, bk=bk, qc=qc: hh_.activation(out=sg[:, qc, :], in_=PS[bk][:, :],
                                                                             func=AF.Silu),
                             reads=[PSB[bk]], writes=[sgb[qc]])
                if ti + 1 < 4:
                    load_rope(lat_tiles[ti + 1])
                    emit_h(lat_tiles[ti + 1], l, b, lambda dc: h[:, dc, :], hb, tscr, tscrb)
                steps = []
                for nl in range(4):
                    nblk = 4 * ti + nl
                    for i in range(2):
                        chunks = [(nblk, None), (16, None), (17, None)]
                        if nblk > 0:
                            chunks.append((nblk - 1, maskP))
                        if nblk < 15:
                            chunks.append((nblk + 1, maskN))
                        tot = 2 * len(chunks)
                        q_ = 0
                        for half in range(2):
                            for (kc, msk) in chunks:
                                steps.append((nl, i, half, kc, msk, q_ == 0, q_ == tot - 1))
                                q_ += 1
                DEP = 3
                stb = {}

                def emit_qk(si):
                    (nl, i, half, kc, msk, first, last) = steps[si]
                    ps_ = slice(0, 64) if half == 0 else slice(64, 128)
                    bk = banks.next()
                    kti = min(kc // 4, 4)
                    K.op(K.pe, lambda hh_: hh_.matmul(
                        PS[bk][:, :], kT[:, i, half, kc * 128:(kc + 1) * 128], qT[:, 4 * i:4 * i + 4, nl * 128:(nl + 1) * 128],
                        start=True, stop=True), reads=[kTb[kti]] + qTb[4 * i:4 * i + 4], writes=[PSB[bk]])
                    p = si % NPT
                    K.op(K.act, lambda hh_: hh_.activation(out=PT[p][:], in_=PS[bk][:, :], func=AF.Exp, scale=0.125),
                         reads=[PSB[bk]], writes=[PTb[p]])
                    if msk is not None:
                        K.op(K.pool, lambda hh_: hh_.tensor_tensor(out=PT[p][:], in0=PT[p][:], in1=msk, op=ALU.mult),
                             reads=[PTb[p], CB], writes=[PTb[p]])

                def emit_pv(si):
                    (nl, i, half, kc, msk, first, last) = steps[si]
                    grp = (nl * 2 + i) % 2
                    nb_k, db_k = 2 * grp, 2 * grp + 1
                    p = si % NPT
                    vs = slice(0, 128) if half == 0 else slice(64, 192)
                    kti = min(kc // 4, 4)
                    K.op(K.pe, lambda hh_: hh_.matmul(PS[nb_k][:, :], VZ[:, kc, i, vs], PT[p][:], start=first, stop=last),
                         reads=[PTb[p], VZb[kti]], writes=[PSB[nb_k]], signal=False)
                    K.op(K.pe, lambda hh_: hh_.matmul(PS[db_k][:, :], OZ[:, vs], PT[p][:], start=first, stop=last),
                         reads=[PTb[p], CB], writes=[PSB[db_k]], signal=True)
                    if last:
                        K.op(K.dve, lambda hh_: hh_.tensor_tensor(
                            out=dn[:].rearrange("p (a q) -> p a q", a=4),
                            in0=PS[db_k][:, :].rearrange("p (a q) -> p a q", a=4),
                            in1=E[:, 4 * i:4 * i + 4].unsqueeze(2).to_broadcast([128, 4, 128]), op=ALU.add),
                            reads=[PSB[db_k], lb], writes=[dnb])
                        K.op(K.dve, lambda hh_: hh_.reciprocal(out=dn[:], in_=dn[:]), reads=[dnb], writes=[dnb])
                        K.op(K.dve, lambda hh_: hh_.tensor_tensor(out=of[:], in0=PS[nb_k][:, :], in1=dn[:], op=ALU.mult),
                             reads=[PSB[nb_k], dnb], writes=[ofb])
                        K.op(K.dve, lambda hh_: hh_.tensor_tensor(
                            out=y[:, 4 * i:4 * i + 4, nl * 128:(nl + 1) * 128],
                            in0=of[:].rearrange("p (a q) -> p a q", a=4),
                            in1=sg[:, 4 * i:4 * i + 4, nl * 128:(nl + 1) * 128], op=ALU.mult),
                            reads=[ofb] + sgb[4 * i:4 * i + 4], writes=yb[4 * i:4 * i + 4])

                ns = len(steps)
                if ASTG < 3:
                    ns = 0

                def qk_deps(si):
                    (nl, i, half, kc, msk, first, last) = steps[si]
                    bk = banks.ids[banks.i % len(banks.ids)]
                    return [kTb[min(kc // 4, 4)]] + qTb[4 * i:4 * i + 4], [PSB[bk]]

                def pv_deps(si):
                    (nl, i, half, kc, msk, first, last) = steps[si]
                    grp = (nl * 2 + i) % 2
                    return [PTb[si % NPT], VZb[min(kc // 4, 4)], CB], [PSB[2 * grp], PSB[2 * grp + 1]]

                for si in range(min(DEP, ns)):
                    emit_qk(si)
                for si in range(ns):
                    if si + DEP < ns:
                        r_, w_ = qk_deps(si + DEP)
                        K._deps(K.pe, r_, w_)
                    r_, w_ = pv_deps(si)
                    K._deps(K.pe, r_, w_)
                    if si + DEP < ns:
                        emit_qk(si + DEP)
                    emit_pv(si)
                for blk in range(2):
                    slot, slb = acquire(base + 5 + blk)
                    so = slot[:].rearrange("p (m c k) -> p m c k", m=4, c=8)
                    for m in range(4):
                        dmc = 4 * blk + m
                        bk = banks.next()
                        K.mm(PSB[bk], PS[bk][:, :], [(so[:, m, qc, :], y[:, qc, :]) for qc in range(8)],
                             reads=yb + [slb])
                        emit_resid(tl, l, b, dmc, bk)
            banks.set(range(8))
            scope_barrier(allb + ringb + PSB)

    def emit_final(b):
        with ExitStack() as st:
            allb = []
            sq = sb(st, "fsq", [128, 16, 512], BF16)
            sd = sb(st, "fsd", [128, 512], F32)
            sdb = Buf()
            emit_rstd(lat_tiles, [lambda dc, n: sq[:, dc, 0:n], lambda dc, n: sq[:, 8 + dc, 0:n]], sd, sdb)
            for ti, tl in enumerate(lat_tiles):
                for dc in range(8):
                    K.op(K.dve, lambda h, tl=tl, dc=dc: h.scalar_tensor_tensor(
                        out=tl.x(dc), in0=tl.x(dc), scalar=fg_t[:, dc:dc + 1], in1=rstd[:, tl.rs0:tl.rs0 + 512],
                        op0=ALU.mult, op1=ALU.mult), reads=[RB[ti], tl.rbuf, CB], writes=[tl.rbuf])
                K.dma(K.sp, outT[b][:, :, 512 * ti:512 * ti + 512].rearrange("c p t -> p c t"),
                      X[:, :, 512 * ti:512 * ti + 512], os_t[ti], reads=[XB[ti]])
            scope_barrier(allb)

    plan_weights()
    for b in range(NB):
        if b > 0:
            load_inputs(b)
        for li, l in enumerate(layers):
            if b == 0 and li + 1 < len(layers):
                convert(layers[li + 1])
            if l == 0:
                emit_gmlp(0, 0, b, lat_tiles + [ctx_tile])
            elif l == 1:
                emit_pool(1, b)
            elif l == 2:
                emit_attn(2, b)
            else:
                emit_gmlp(3, 1, b, lat_tiles)
        emit_final(b)
    for t in range(4):
        K.sp.h.wait_ge(os_t[t].h, os_t[t].cnt)
    assert wstate["used"] == len(wseq), (wstate, len(wseq))
    for e in (K.pe, K.act, K.dve, K.pool):
        assert e.cnt < 60000, (e.name, e.cnt)
    es.close()
    return nc, K


def _qperm():
    cols = []
    for i in range(2):
        for a in range(4):
            hlo = 4 * (2 * i) + a
            hhi = 4 * (2 * i + 1) + a
            cols += list(range(hlo * 64, hlo * 64 + 64)) + list(range(hhi * 64, hhi * 64 + 64))
    return np.array(cols)


def _win_blocks(W):
    ncc = W.shape[1] // 128
    t = W.reshape(8, 128, ncc, 128).transpose(2, 1, 0, 3)
    t = t.reshape(ncc // 4, 4, 128, 8, 128).transpose(0, 2, 1, 3, 4)
    return np.ascontiguousarray(t).reshape(ncc // 4, 128, 4096)


def _rope_tables():
    import jax
    import jax.numpy as jnp
    with jax.default_device(jax.devices("cpu")[0]):
        rows = S // 64
        row = jnp.repeat(jnp.arange(rows), 64).astype(jnp.float32)
        col = jnp.tile(jnp.arange(64), rows).astype(jnp.float32)
        nf = 16
        inv = 10000.0 ** (-jnp.arange(nf, dtype=jnp.float32) / nf)
        ang = jnp.concatenate([row[:, None] * inv, col[:, None] * inv], axis=-1)
        cos = np.asarray(jnp.cos(ang)).T.astype(np.float32)
        sin = np.asarray(jnp.sin(ang)).T.astype(np.float32)
    c = np.tile(cos, (4, 1))
    s = np.concatenate([-sin, sin, -sin, sin], axis=0)
    return np.ascontiguousarray(c), np.ascontiguousarray(s)


def _shared_layout(inp):
    f = lambda a: np.ascontiguousarray(np.asarray(a, dtype=np.float32))
    d = {}
    d["adaw"] = f(inp["ada_w"].reshape(4, 8, 128, 24, 128).transpose(0, 3, 2, 1, 4))
    d["adab"] = f(inp["ada_b"].reshape(4, 24, 128).transpose(2, 0, 1))
    d["ng"] = f(inp["norm_g"].reshape(4, 8, 128).transpose(2, 0, 1))
    d["fg"] = f(inp["final_g"].reshape(8, 128).T)
    blocks = []
    qp = _qperm()
    for j, l in ((0, 0),):
        pass
    def gm(j):
        out = [_win_blocks(inp["gmlp_w_in"][j])]
        W = inp["gmlp_w_out"][j]
        t = W.reshape(16, 128, 8, 128).transpose(2, 1, 0, 3)
        t = t.reshape(4, 2, 128, 16, 128).transpose(0, 2, 1, 3, 4)
        out.append(np.ascontiguousarray(t).reshape(4, 128, 4096))
        return out
    blocks += gm(0)
    blocks.append(_win_blocks(inp["pool_w_in"][0]))
    wg = inp["pool_w_grp"][0]
    t = wg.reshape(4, 4, 128, 4, 128).transpose(0, 2, 3, 1, 4)
    t = np.ascontiguousarray(t).reshape(4, 128, 2048)
    blocks.append(np.concatenate([t, np.zeros_like(t)], axis=2))
    W = inp["pool_w_out"][0]
    t = W.reshape(4, 4, 128, 8, 128).transpose(0, 2, 3, 1, 4)
    blocks.append(np.ascontiguousarray(t).reshape(4, 128, 4096))
    W = inp["attn_w_in"][0]
    order = np.concatenate([qp, np.arange(1024, 1536), 1536 + qp])
    blocks.append(_win_blocks(np.ascontiguousarray(W[:, order])))
    W = inp["attn_w_out"][0][qp, :]
    t = W.reshape(8, 128, 8, 128).transpose(2, 1, 0, 3)
    t = t.reshape(2, 4, 128, 8, 128).transpose(0, 2, 1, 3, 4)
    blocks.append(np.ascontiguousarray(t).reshape(2, 128, 4096))
    blocks += gm(1)
    d["wts"] = f(np.concatenate(blocks, axis=0))
    assert d["wts"].shape == (NBLK, 128, 4096)
    d["g_wsT"] = f(inp["gmlp_w_s"].transpose(0, 3, 1, 2))
    d["g_bsbc"] = f(np.broadcast_to(inp["gmlp_b_s"][:, None, :, :], (2, 128, 8, 128)))
    d["g_vng"] = f(inp["gmlp_vnorm_g"].reshape(2, 16, 128).transpose(0, 2, 1))
    d["g_vnb"] = f(inp["gmlp_vnorm_b"].reshape(2, 16, 128).transpose(0, 2, 1))
    d["p_scale"] = f(inp["pool_scale"][0].reshape(16, 128).T)
    ic = np.zeros((128, 4, 2, 8), np.float32)
    for g, w in enumerate(POOL_W):
        for t_ in range(8):
            ic[:, g, 0, t_] = np.float32(1.0) / np.float32(min(t_ + w // 2, S) - max(t_ - w // 2, 0))
            tt = S - 8 + t_
            ic[:, g, 1, t_] = np.float32(1.0) / np.float32(min(tt + w // 2, S) - max(tt - w // 2, 0))
    d["p_icnt"] = ic
    sk = inp["attn_sink"][0]
    e = np.zeros((128, 8), np.float32)
    for i in range(2):
        for a in range(4):
            e[0:64, 4 * i + a] = sk[4 * (2 * i) + a]
            e[64:128, 4 * i + a] = sk[4 * (2 * i + 1) + a]
    d["a_sink"] = e
    d["rope_c"], d["rope_s"] = _rope_tables()
    cst = np.zeros((128, 1152), np.float32)
    for m in range(128):
        partner = m + 32 if (m % 64) < 32 else m - 32
        cst[partner, m] = 1.0
    jj = np.arange(128)[:, None]
    ii = np.arange(128)[None, :]
    cst[:, 128:640] = np.tile((ii <= jj).astype(np.float32), (1, 4))
    cst[:, 640:1152] = np.tile((jj <= ii).astype(np.float32), (1, 4))
    d["cstf"] = cst
    return d


def _core_layout(inp, b0, NB):
    f = lambda a: np.ascontiguousarray(np.asarray(a, dtype=np.float32))
    d = {}
    d["xT"] = f(inp["x"][b0:b0 + NB].transpose(0, 2, 1).reshape(NB, 8, 128, S))
    d["cT"] = f(inp["ctx"][b0:b0 + NB].transpose(0, 2, 1).reshape(NB, 8, 128, LC))
    cc = np.concatenate([inp["c"][b0:b0 + NB], inp["c_ctx"][None, :]], axis=0)
    d["condT"] = f(cc.reshape(NB + 1, 8, 128).transpose(2, 1, 0))
    return d


_PROG = {}


def kernel(**inputs):
    inp = {k: np.asarray(v) for k, v in inputs.items()}
    NB = 4
    if "p" not in _PROG:
        _PROG["p"] = build_program(NB)[0]
    nc = _PROG["p"]
    shared = _shared_layout(inp)
    in_maps = []
    for c in range(8):
        m = dict(shared)
        m.update(_core_layout(inp, c * NB, NB))
        in_maps.append(m)
    res = run_bass_kernel_spmd(nc, in_maps, core_ids=list(range(8)))
    outs = []
    for c in range(8):
        o = np.asarray(res.results[c]["outT"]).reshape(NB, D, S).transpose(0, 2, 1)
        outs.append(o)
    return np.ascontiguousarray(np.concatenate(outs, axis=0)).astype(np.float32)
```

```python
import numpy as np
from contextlib import ExitStack
import concourse.bass as bass
import concourse.mybir as mybir
from concourse.bass_utils import run_bass_kernel_spmd

F32 = mybir.dt.float32
BF16 = mybir.dt.bfloat16
AF = mybir.ActivationFunctionType
ALU = mybir.AluOpType

S = 2048
LC = 256
D = 1024
NLAYER = 4
EPS = 1e-6
NBLK = 55
BLK_BASE = {0: 0, 1: 16, 2: 32, 3: 39}
POOL_W = (2, 4, 8, 16)


class Eng:
    def __init__(self, name, h, sem, is_pe=False):
        self.name, self.h, self.sem, self.is_pe = name, h, sem, is_pe
        self.cnt = 0
        self.waited = {}


class DSem:
    def __init__(self, h):
        self.h = h
        self.cnt = 0


class Buf:
    __slots__ = ("w", "r")

    def __init__(self):
        self.w = None
        self.r = {}


class Sched:
    def __init__(self, nc, es):
        self.nc, self.es = nc, es
        self.nsem = 0
        self.pe = Eng("pe", nc.tensor, self.sem("s_pe"), True)
        self.act = Eng("act", nc.scalar, self.sem("s_act"))
        self.dve = Eng("dve", nc.vector, self.sem("s_dve"))
        self.pool = Eng("pool", nc.gpsimd, self.sem("s_pool"))
        self.sp = Eng("sp", nc.sync, self.sem("s_sp"))
        self.nins = 0

    def sem(self, name):
        self.nsem += 1
        return self.es.enter_context(self.nc.semaphore(name))

    def dsem(self, name):
        return DSem(self.sem(name))

    def _deps(self, eng, reads, writes):
        for b in reads:
            if b.w is not None:
                self._wait(eng, b.w, True)
        for b in writes:
            if b.w is not None:
                self._wait(eng, b.w, False)
            for t in b.r.values():
                self._wait(eng, t, False)

    def _wait(self, eng, tok, raw):
        sem, val, src = tok
        if src is eng and (eng.is_pe or not raw):
            return
        key = id(sem)
        if eng.waited.get(key, 0) >= val:
            return
        eng.h.wait_ge(sem, val)
        eng.waited[key] = val
        self.nins += 1

    @staticmethod
    def _commit(tok, reads, writes):
        for b in writes:
            b.w = tok
            b.r = {}
        k = id(tok[0])
        for b in reads:
            o = b.r.get(k)
            if o is None or o[1] < tok[1]:
                b.r[k] = tok

    def op(self, eng, fn, reads=(), writes=(), signal=True):
        self._deps(eng, reads, writes)
        ins = fn(eng.h)
        self.nins += 1
        if signal:
            eng.cnt += 1
            ins.then_inc(eng.sem, 1)
            tok = (eng.sem, eng.cnt, eng)
        else:
            tok = (eng.sem, eng.cnt + 1, eng)
        self._commit(tok, reads, writes)

    def dma(self, q, out, in_, dsem, reads=(), writes=()):
        self.dma_group(q, [(out, in_)], dsem, reads, writes)

    def dma_group(self, q, pairs, dsem, reads=(), writes=()):
        self._deps(q, reads, writes)
        for (o, i) in pairs:
            q.h.dma_start(out=o, in_=i).then_inc(dsem.h, 16)
            dsem.cnt += 16
            self.nins += 1
        tok = (dsem.h, dsem.cnt, None)
        self._commit(tok, reads, writes)

    def barrier(self):
        engs = (self.pe, self.act, self.dve, self.pool)
        for d in engs + (self.sp,):
            for e in engs:
                if e is not d and e.cnt > 0:
                    self._wait(d, (e.sem, e.cnt, e), True)

    def mm(self, bank, out_ap, pairs, reads):
        n = len(pairs)
        for k, (l, r) in enumerate(pairs):
            self.op(self.pe,
                    lambda h, l=l, r=r, k=k: h.matmul(out_ap, l, r, start=(k == 0), stop=(k == n - 1)),
                    reads=reads if k == 0 else (), writes=[bank] if k == 0 else (), signal=(k == n - 1))


class Tl:
    def __init__(self, res, rbuf, c0, n, ci, rs0):
        self.res, self.rbuf, self.c0, self.n, self.ci, self.rs0 = res, rbuf, c0, n, ci, rs0

    def x(self, dc, a=0, n=None):
        n = self.n if n is None else n
        return self.res[:, dc, self.c0 + a:self.c0 + a + n]


def build_program(NB=4, layers=(0, 1, 2, 3)):
    nc = bass.Bass("TRN2", target_bir_lowering=False)
    es = ExitStack()
    K = Sched(nc, es)
    NC = NB + 1

    def din(name, shape, dt=F32):
        return nc.dram_tensor(name, list(shape), dt, kind="ExternalInput").ap()

    xT = din("xT", [NB, 8, 128, S])
    cT = din("cT", [NB, 8, 128, LC])
    condT = din("condT", [128, 8, NC])
    adaw = din("adaw", [NLAYER, 24, 128, 8, 128])
    adab = din("adab", [128, NLAYER, 24])
    ngd = din("ng", [128, NLAYER, 8])
    fgd = din("fg", [128, 8])
    wts = din("wts", [NBLK, 128, 4096])
    g_wsT = din("g_wsT", [2, 128, 8, 128])
    g_bsbc = din("g_bsbc", [2, 128, 8, 128])
    g_vng = din("g_vng", [2, 128, 16])
    g_vnb = din("g_vnb", [2, 128, 16])
    p_scale = din("p_scale", [128, 16])
    p_icnt = din("p_icnt", [128, 4, 2, 8])
    a_sink = din("a_sink", [128, 8])
    rope_c = din("rope_c", [128, S])
    rope_s = din("rope_s", [128, S])
    cstf_d = din("cstf", [128, 1152])
    outT = nc.dram_tensor("outT", [NB, 8, 128, S], F32, kind="ExternalOutput").ap()
    wtb = nc.dram_tensor("wtb", [NBLK, 128, 4096], BF16).ap()
    import os as _os
    DBG = _os.environ.get("KDBG") == "1"
    if DBG:
        dbgf = nc.dram_tensor("dbgf", [128, 4096], F32, kind="ExternalOutput").ap()
        dbgb = nc.dram_tensor("dbgb", [128, 4096], BF16, kind="ExternalOutput").ap()
        dbs = K.dsem("dbs")

    _uid = [0]

    def sb(st, name, shape, dt):
        _uid[0] += 1
        return st.enter_context(nc.sbuf_tensor("%s_%d" % (name, _uid[0]), list(shape), dt))

    PS = [es.enter_context(nc.psum_tensor(f"ps{k}", [128, 512], F32)) for k in range(8)]
    PSB = [Buf() for _ in range(8)]

    class BankRing:
        def __init__(self):
            self.ids = list(range(8))
            self.i = 0

        def set(self, ids):
            self.ids = list(ids)
            self.i = 0

        def next(self):
            b = self.ids[self.i % len(self.ids)]
            self.i += 1
            return b

    banks = BankRing()

    eps_t = sb(es, "eps_t", [128, 1], F32)
    ones_mean = sb(es, "ones_mean", [128, 128], BF16)
    ones_f = sb(es, "ones_f", [128, 128], F32)
    OZ = sb(es, "OZ", [128, 192], BF16)
    cstb = sb(es, "cstb", [128, 1152], BF16)
    adab_t = sb(es, "adab_t", [128, NLAYER, 24], F32)
    ng_t = sb(es, "ng_t", [128, NLAYER, 8], F32)
    fg_t = sb(es, "fg_t", [128, 8], F32)
    mod = sb(es, "mod", [128, NLAYER, 24, NC], F32)
    GF = sb(es, "GF", [128, NLAYER, NC, 8], F32)
    CB = Buf()
    MODB = Buf()
    GFB = Buf()

    NSLOT = 4
    ring = [sb(es, f"wr{k}", [128, 4096], BF16) for k in range(NSLOT)]
    ringb = [Buf() for _ in range(NSLOT)]
    rings = [K.dsem(f"wrs{k}") for k in range(NSLOT)]
    convb = {bi: Buf() for bi in range(NBLK)}
    convs = {bi: K.dsem(f"cv{bi}") for bi in range(NBLK)}

    def blk_layer(bi):
        for l in (3, 2, 1, 0):
            if bi >= BLK_BASE[l]:
                return l

    wseq = []
    wstate = {"issued": 0, "used": 0}

    def plan_weights():
        for b in range(NB):
            for l in layers:
                base = BLK_BASE[l]
                if l in (0, 3):
                    ntile = 5 if l == 0 else 4
                    for _ in range(ntile):
                        wseq.extend([base + 4 + c for c in range(4)])
                        wseq.extend([base + c for c in range(4)])
                        wseq.extend([base + 8 + c for c in range(4)])
                        wseq.extend([base + 12 + c for c in range(4)])
                elif l == 1:
                    for hf in range(2):
                        for g in range(4):
                            wseq.extend([base + g, base + 4 + g, base + 8 + g, base + 12 + g])
                else:
                    for _ in range(5):
                        wseq.append(base + 2)
                    wseq.extend([base + 0, base + 1, base + 3, base + 4])
                    for t_ in range(4):
                        if t_ + 1 < 4:
                            wseq.extend([base + 0, base + 1])
                        wseq.extend([base + 5, base + 6])
                        if t_ + 1 < 4:
                            wseq.extend([base + 3, base + 4])

    def acquire(bi):
        i = wstate["used"]
        assert wseq[i] == bi, (i, wseq[i], bi)
        wstate["used"] += 1
        while wstate["issued"] < min(len(wseq), i + NSLOT - 1):
            k = wstate["issued"]
            s = k % NSLOT
            K.dma(K.sp, ring[s][:], wtb[wseq[k]], rings[s], reads=[convb[wseq[k]]], writes=[ringb[s]])
            wstate["issued"] += 1
        s = i % NSLOT
        return ring[s], ringb[s]

    X = sb(es, "X", [128, 8, S], F32)
    C = sb(es, "C", [128, 8, LC], F32)
    rstd = sb(es, "rstd", [128, S + LC], F32)
    XB = [Buf() for _ in range(4)]
    CBUF = Buf()
    RB = [Buf() for _ in range(5)]
    xs = K.dsem("xld")
    xs_t = [K.dsem("xld%d" % t) for t in range(4)]
    os_t = [K.dsem("ost%d" % t) for t in range(4)]
    lcs = [K.dsem("lc0"), K.dsem("lc1")]
    lcn = [0]

    def lc_sem():
        lcn[0] += 1
        return lcs[lcn[0] % 2]

    lat_tiles = [Tl(X, XB[t], 512 * t, 512, None, 512 * t) for t in range(4)]
    ctx_tile = Tl(C, CBUF, 0, LC, NB, S)

    def load_inputs(b):
        for t in range(4):
            K.dma(K.sp, X[:, :, 512 * t:512 * t + 512], xT[b][:, :, 512 * t:512 * t + 512].rearrange("c p t -> p c t"),
                  xs_t[t], writes=[XB[t]])
        K.dma_group(K.sp, [(C[:], cT[b].rearrange("c p t -> p c t"))], xs, writes=[CBUF])

    load_inputs(0)

    def convert(l):
        base = BLK_BASE[l]
        if l in (0, 3):
            order = [4, 5, 6, 7, 0, 1, 2, 3] + list(range(8, 16))
        elif l == 1:
            order = [g + 4 * q for g in range(4) for q in range(4)]
        else:
            order = [2, 0, 1, 3, 4, 5, 6]
        for o in order:
            K.dma(K.pool, wtb[base + o], wts[base + o], convs[base + o], writes=[convb[base + o]])

    convert(layers[0]) if layers else None

    cs = K.dsem("cst")
    with ExitStack() as ps_:
        cstf = sb(ps_, "cstf_t", [128, 1152], F32)
        cond = sb(ps_, "cond", [128, 8, NC], F32)
        conds = sb(ps_, "conds", [128, 8, NC], F32)
        stage = [sb(ps_, f"adst{k}", [128, 4, 8, 128], F32) for k in range(2)]
        stageb = [Buf(), Buf()]
        stages = [K.dsem("adst_s0"), K.dsem("adst_s1")]
        tb = Buf()
        K.dma_group(K.sp, [(cstf[:], cstf_d), (cond[:], condT), (adab_t[:], adab), (ng_t[:], ngd), (fg_t[:], fgd)],
                    cs, writes=[tb])
        K.op(K.dve, lambda h: h.memset(eps_t[:], EPS), writes=[CB])
        K.op(K.dve, lambda h: h.memset(ones_mean[:], 1.0 / 1024.0), writes=[CB])
        K.op(K.dve, lambda h: h.memset(ones_f[:], 1.0), writes=[CB])
        K.op(K.dve, lambda h: h.memset(OZ[:], 1.0), writes=[CB])
        K.op(K.dve, lambda h: h.memset(OZ[:, 64:128], 0.0), writes=[CB])
        K.op(K.dve, lambda h: h.tensor_copy(out=cstb[:], in_=cstf[:]), reads=[tb], writes=[CB])
        condb = Buf()
        K.op(K.act, lambda h: h.activation(out=conds[:], in_=cond[:], func=AF.Silu), reads=[tb], writes=[condb])
        k = 0
        for l in layers:
            for t6 in range(6):
                s = k % 2
                k += 1
                K.dma(K.sp, stage[s][:], adaw[l, 4 * t6:4 * t6 + 4].rearrange("c p d m -> p c d m"), stages[s],
                      writes=[stageb[s]])
                for m in range(4):
                    cc = 4 * t6 + m
                    bk = banks.next()
                    K.mm(PSB[bk], PS[bk][:, 0:NC], [(stage[s][:, m, dc, :], conds[:, dc, :]) for dc in range(8)],
                         reads=[stageb[s], condb])
                    K.op(K.dve, lambda h, bk=bk, l=l, cc=cc: h.tensor_scalar(
                        out=mod[:, l, cc, :], in0=PS[bk][:, 0:NC], scalar1=adab_t[:, l, cc:cc + 1], scalar2=None,
                        op0=ALU.add), reads=[PSB[bk], tb], writes=[MODB])
        for l in layers:
            for n in range(NC):
                K.op(K.dve, lambda h, l=l, n=n: h.tensor_scalar(
                    out=GF[:, l, n, :], in0=mod[:, l, 8:16, n], scalar1=1.0, scalar2=None, op0=ALU.add),
                    reads=[MODB], writes=[GFB])
                K.op(K.dve, lambda h, l=l, n=n: h.tensor_tensor(
                    out=GF[:, l, n, :], in0=GF[:, l, n, :], in1=ng_t[:, l, :], op=ALU.mult),
                    reads=[GFB, tb], writes=[GFB])
        K.barrier()
    Pm = cstb[:, 0:128]
    maskP = cstb[:, 128:640]
    maskN = cstb[:, 640:1152]

    def scope_barrier(allb):
        K.barrier()

    def emit_rstd(tiles, sqs, sd, sdb):
        sqbs = [[Buf() for _ in range(8)] for _ in range(2)]

        def squares(t):
            tl = tiles[t]
            n = tl.n
            sq = sqs[t % 2]
            for dc in range(8):
                e = (K.act, K.dve, K.pool)[dc % 3] if dc < 6 else (K.act, K.dve)[dc % 2]
                if e is K.act:
                    K.op(e, lambda h, dc=dc: h.activation(out=sq(dc, n), in_=tl.x(dc), func=AF.Square),
                         reads=[tl.rbuf], writes=[sqbs[t % 2][dc]])
                else:
                    K.op(e, lambda h, dc=dc: h.tensor_tensor(out=sq(dc, n), in0=tl.x(dc), in1=tl.x(dc),
                                                             op=ALU.mult), reads=[tl.rbuf], writes=[sqbs[t % 2][dc]])

        squares(0)
        for t, tl in enumerate(tiles):
            n = tl.n
            sq = sqs[t % 2]
            if t + 1 < len(tiles):
                squares(t + 1)
            bk = banks.next()
            K.mm(PSB[bk], PS[bk][:, 0:n], [(ones_mean[:], sq(dc, n)) for dc in range(8)], reads=sqbs[t % 2] + [CB])
            K.op(K.act, lambda h, bk=bk, n=n: h.activation(out=sd[:, 0:n], in_=PS[bk][:, 0:n], func=AF.Sqrt,
                                                            bias=eps_t[:], scale=1.0),
                 reads=[PSB[bk], CB], writes=[sdb])
            ri = tl.rs0 // 512
            K.op(K.dve, lambda h, tl=tl, n=n: h.reciprocal(out=rstd[:, tl.rs0:tl.rs0 + n], in_=sd[:, 0:n]),
                 reads=[sdb], writes=[RB[ri]])
        K.barrier()

    def emit_h(tl, l, ci, hdst, hb, tscr, tscrb, a=0, n=None):
        n = tl.n if n is None else n
        ri = tl.rs0 // 512
        for dc in range(8):
            k = dc % len(tscr)
            K.op(K.dve, lambda h, dc=dc, k=k: h.tensor_tensor(
                out=tscr[k][:, 0:n], in0=tl.x(dc, a, n), in1=rstd[:, tl.rs0 + a:tl.rs0 + a + n], op=ALU.mult),
                reads=[tl.rbuf, RB[ri]], writes=[tscrb[k]])
            K.op(K.act, lambda h, dc=dc, k=k: h.activation(
                out=hdst(dc), in_=tscr[k][:, 0:n], func=AF.Identity, bias=mod[:, l, dc, ci:ci + 1],
                scale=GF[:, l, ci, dc:dc + 1]), reads=[tscrb[k], MODB, GFB], writes=[hb])

    def emit_resid(tl, l, ci, dmc, bk, a=0, n=None):
        n = tl.n if n is None else n
        K.op(K.dve, lambda h: h.scalar_tensor_tensor(
            out=tl.x(dmc, a, n), in0=PS[bk][:, 0:n], scalar=mod[:, l, 16 + dmc, ci:ci + 1], in1=tl.x(dmc, a, n),
            op0=ALU.mult, op1=ALU.add), reads=[PSB[bk], MODB], writes=[tl.rbuf])

    def emit_gmlp(l, j, b, tiles):
        base = BLK_BASE[l]
        with ExitStack() as st:
            allb = []

            def nb_():
                bb = Buf()
                allb.append(bb)
                return bb
            wsT_f = sb(st, "wsT_f", [128, 8, 128], F32)
            bsbc = sb(st, "bsbc", [128, 8, 128], F32)
            vng = sb(st, "vng", [128, 16], F32)
            vnb = sb(st, "vnb", [128, 16], F32)
            wsT_b = sb(st, "wsT_b", [128, 8, 128], BF16)
            Bmat = sb(st, "Bmat", [128, 16, 128], F32)
            hh = [sb(st, f"gh{k}", [128, 8, 512], BF16) for k in range(2)]
            hhb = [nb_(), nb_()]
            tscr = [sb(st, f"gts{k}", [128, 512], F32) for k in range(2)]
            tscrb = [nb_(), nb_()]
            gv = sb(st, "gv", [128, 4, 2048], BF16)
            gvb = [nb_() for _ in range(4)]
            u = sb(st, "gu", [128, 16, 512], BF16)
            ub = [nb_() for _ in range(16)]
            sgr = [sb(st, f"gsg{k}", [128, 512], BF16) for k in range(3)]
            sgb = [nb_() for _ in range(3)]
            t1 = [sb(st, f"gt1{k}", [128, 512], BF16) for k in range(2)]
            t1b = [nb_(), nb_()]
            stt = sb(st, "gstt", [128, 4, 24], F32)
            sttb = nb_()
            mv = sb(st, "gmv", [128, 4, 2], F32)
            mvb = nb_()
            sdv = sb(st, "gsdv", [128, 4], F32)
            rsv = sb(st, "grsv", [128, 4], F32)
            rsvb = nb_()
            sd = sb(st, "gsd", [128, 512], F32)
            sdb = nb_()
            lb = nb_()
            bmb = nb_()
            K.dma_group(K.sp, [(wsT_f[:], g_wsT[j]), (bsbc[:], g_bsbc[j]), (vng[:], g_vng[j]), (vnb[:], g_vnb[j])],
                        lc_sem(), writes=[lb])
            K.op(K.dve, lambda h: h.tensor_copy(out=wsT_b[:], in_=wsT_f[:]), reads=[lb], writes=[bmb])
            bk2 = []
            for hf in range(2):
                bk = banks.next()
                bk2.append(bk)
                K.mm(PSB[bk], PS[bk][:, :], [(ones_f[:], wsT_f[:, 4 * hf:4 * hf + 4, :])], reads=[lb, CB])
            for cc in range(16):
                g = cc // 2
                bk = bk2[g // 4]
                K.op(K.dve, lambda h, cc=cc, g=g, bk=bk: h.scalar_tensor_tensor(
                    out=Bmat[:, cc, :], in0=PS[bk][:, (g % 4) * 128:(g % 4) * 128 + 128], scalar=vnb[:, cc:cc + 1],
                    in1=bsbc[:, g, :], op0=ALU.mult, op1=ALU.add), reads=[PSB[bk], lb], writes=[bmb])
            emit_rstd(tiles, [lambda dc, n: u[:, dc, 0:n], lambda dc, n: u[:, 8 + dc, 0:n]], sd, sdb)

            def cidx(tl):
                return tl.ci if tl.ci is not None else b

            def hd(k, n):
                return lambda dc: hh[k][:, dc, 0:n]
            emit_h(tiles[0], l, cidx(tiles[0]), hd(0, tiles[0].n), hhb[0], tscr, tscrb)
            for ti, tl in enumerate(tiles):
                n = tl.n
                nj = n // 128
                ci = cidx(tl)
                h = hh[ti % 2]
                hb = hhb[ti % 2]
                for cg in range(4):
                    slot, slb = acquire(base + 4 + cg)
                    sv = slot[:].rearrange("p (m d k) -> p m d k", m=4, d=8)
                    for jj in range(nj):
                        bk = banks.next()
                        K.mm(PSB[bk], PS[bk][:, :],
                             [(h[:, dc, jj * 128:(jj + 1) * 128], sv[:, :, dc, :]) for dc in range(8)], reads=[hb, slb])
                        K.op(K.act, lambda hh_, bk=bk, jj=jj, cg=cg: hh_.activation(
                            out=gv[:, jj, cg * 512:(cg + 1) * 512], in_=PS[bk][:, :], func=AF.Gelu_apprx_tanh),
                            reads=[PSB[bk]], writes=[gvb[jj]])
                        K.op(K.dve, lambda hh_, jj=jj, cg=cg: hh_.bn_stats(
                            out=stt[:, jj, cg * 6:(cg + 1) * 6], in_=gv[:, jj, cg * 512:(cg + 1) * 512]),
                            reads=[gvb[jj]], writes=[sttb])
                for jj in range(nj):
                    K.op(K.dve, lambda hh_, jj=jj: hh_.bn_aggr(out=mv[:, jj, :], in_=stt[:, jj, :]),
                         reads=[sttb], writes=[mvb])
                K.op(K.act, lambda hh_: hh_.activation(out=sdv[:, 0:nj], in_=mv[:, 0:nj, 1], func=AF.Sqrt,
                                                       bias=eps_t[:], scale=1.0), reads=[mvb, CB], writes=[rsvb])
                K.op(K.dve, lambda hh_: hh_.reciprocal(out=rsv[:, 0:nj], in_=sdv[:, 0:nj]), reads=[rsvb], writes=[rsvb])
                for jj in range(nj):
                    K.op(K.dve, lambda hh_, jj=jj: hh_.tensor_scalar(
                        out=gv[:, jj, :], in0=gv[:, jj, :], scalar1=mv[:, jj, 0:1], scalar2=rsv[:, jj:jj + 1],
                        op0=ALU.subtract, op1=ALU.mult), reads=[gvb[jj], mvb, rsvb], writes=[gvb[jj]])
                for cg in range(4):
                    slot, slb = acquire(base + cg)
                    sv = slot[:].rearrange("p (m d k) -> p m d k", m=4, d=8)
                    for m in range(4):
                        cc = 4 * cg + m
                        bk = banks.next()
                        K.mm(PSB[bk], PS[bk][:, 0:n], [(sv[:, m, dc, :], h[:, dc, 0:n]) for dc in range(8)],
                             reads=[hb, slb])
                        K.op(K.act, lambda hh_, bk=bk, cc=cc: hh_.activation(
                            out=u[:, cc, 0:n], in_=PS[bk][:, 0:n], func=AF.Gelu_apprx_tanh),
                            reads=[PSB[bk]], writes=[ub[cc]])
                def spatial(cc):
                    g = cc // 2
                    bk = banks.next()
                    for jj in range(nj):
                        K.op(K.pe, lambda hh_, bk=bk, jj=jj, cc=cc, g=g: hh_.matmul(
                            PS[bk][:, jj * 128:(jj + 1) * 128], gv[:, jj, cc * 128:(cc + 1) * 128], wsT_b[:, g, :],
                            start=True, stop=True), reads=[gvb[jj], bmb], writes=[PSB[bk]], signal=(jj == nj - 1))
                    k2 = cc % 2
                    K.op(K.dve, lambda hh_, bk=bk, cc=cc, k2=k2: hh_.scalar_tensor_tensor(
                        out=t1[k2][:, 0:n].rearrange("p (j q) -> p j q", q=128),
                        in0=PS[bk][:, 0:n].rearrange("p (j q) -> p j q", q=128), scalar=vng[:, cc:cc + 1],
                        in1=Bmat[:, cc, :].unsqueeze(1).to_broadcast([128, nj, 128]), op0=ALU.mult, op1=ALU.add),
                        reads=[PSB[bk], lb, bmb], writes=[t1b[k2]])
                    K.op(K.dve, lambda hh_, cc=cc, k2=k2: hh_.tensor_tensor(
                        out=u[:, cc, 0:n], in0=t1[k2][:, 0:n], in1=u[:, cc, 0:n], op=ALU.mult),
                        reads=[t1b[k2], ub[cc]], writes=[ub[cc]])

                kk = 0
                for cg in range(4):
                    slot, slb = acquire(base + 8 + cg)
                    sv = slot[:].rearrange("p (m d k) -> p m d k", m=4, d=8)
                    for m in range(4):
                        cc = 4 * cg + m
                        bk = banks.next()
                        s3 = kk % 3
                        kk += 1
                        K.mm(PSB[bk], PS[bk][:, 0:n], [(sv[:, m, dc, :], h[:, dc, 0:n]) for dc in range(8)],
                             reads=[hb, slb])
                        K.op(K.act, lambda hh_, bk=bk, s3=s3: hh_.activation(
                            out=sgr[s3][:, 0:n], in_=PS[bk][:, 0:n], func=AF.Silu), reads=[PSB[bk]], writes=[sgb[s3]])
                        K.op(K.pool, lambda hh_, cc=cc, s3=s3: hh_.tensor_tensor(
                            out=u[:, cc, 0:n], in0=u[:, cc, 0:n], in1=sgr[s3][:, 0:n], op=ALU.mult),
                            reads=[ub[cc], sgb[s3]], writes=[ub[cc]])
                        if cc >= 2:
                            spatial(cc - 2)
                spatial(14)
                spatial(15)
                if ti + 1 < len(tiles):
                    nt = tiles[ti + 1]
                    emit_h(nt, l, cidx(nt), hd((ti + 1) % 2, nt.n), hhb[(ti + 1) % 2], tscr, tscrb)
                for dg in range(4):
                    slot, slb = acquire(base + 12 + dg)
                    so = slot[:].rearrange("p (m c k) -> p m c k", m=2, c=16)
                    for m in range(2):
                        dmc = 2 * dg + m
                        bk = banks.next()
                        for cc in range(16):
                            K.op(K.pe, lambda hh_, bk=bk, m=m, cc=cc: hh_.matmul(
                                PS[bk][:, 0:n], so[:, m, cc, :], u[:, cc, 0:n], start=(cc == 0), stop=(cc == 15)),
                                reads=[ub[cc], slb], writes=[PSB[bk]], signal=(cc == 15))
                        emit_resid(tl, l, ci, dmc, bk)
            scope_barrier(allb + ringb)

    def emit_pool(l, b):
        base = BLK_BASE[l]
        with ExitStack() as st:
            allb = []

            def nb_():
                bb = Buf()
                allb.append(bb)
                return bb
            HW = 1032 + LC
            hh = sb(st, "ph", [128, 8, HW], BF16)
            hbA, hb0, hb1, hb2 = nb_(), nb_(), nb_(), nb_()
            hhalo = sb(st, "phalo", [128, 8, 8], BF16)
            hhalob = nb_()
            tscr = [sb(st, f"pts{k}", [128, 512], F32) for k in range(2)]
            tscrb = [nb_(), nb_()]
            PL = [sb(st, f"pPL{k}", [128, 1048], F32) for k in range(2)]
            PLb = [nb_(), nb_()]
            PC = [sb(st, f"pPC{k}", [128, 272], F32) for k in range(2)]
            PCb = [nb_(), nb_()]
            T = [sb(st, f"pT{k}", [128, 1048], F32) for k in range(2)]
            Tb = [nb_(), nb_()]
            TC = [sb(st, f"pTC{k}", [128, 272], F32) for k in range(2)]
            TCb = [nb_(), nb_()]
            pooled = sb(st, "ppool", [128, 4, 1024 + LC], BF16)
            poolb = [nb_() for _ in range(4)]
            y = sb(st, "py", [128, 4, 1024 + LC], BF16)
            yb = [nb_() for _ in range(4)]
            sgp = sb(st, "psg", [128, 4, 1024 + LC], BF16)
            sgpb = [nb_() for _ in range(4)]
            psc = sb(st, "psc", [128, 16], F32)
            icnt = sb(st, "picnt", [128, 4, 2, 8], F32)
            bfix = sb(st, "pbfix", [128, 8], F32)
            bfixb = nb_()
            sq = sb(st, "psq", [128, 8, 512], BF16)
            sd = sb(st, "psd", [128, 512], F32)
            sdb = nb_()
            lb = nb_()
            K.dma_group(K.sp, [(psc[:], p_scale), (icnt[:], p_icnt)], lc_sem(), writes=[lb])
            for k in range(2):
                K.op(K.dve, lambda h, k=k: h.memset(PL[k][:], 0.0), writes=[PLb[k]])
                K.op(K.dve, lambda h, k=k: h.memset(PC[k][:], 0.0), writes=[PCb[k]])
            emit_rstd(lat_tiles + [ctx_tile], [lambda dc, n: sq[:, dc, 0:n],
                                                 lambda dc, n: pooled[:, dc // 2, (dc % 2) * 512:(dc % 2) * 512 + n]], sd, sdb)
            emit_h(lat_tiles[1], l, b, lambda dc: hhalo[:, dc, :], hhalob, tscr, tscrb, a=504, n=8)
            pk = 0
            for hf in range(2):
                t0_, t1_ = lat_tiles[2 * hf], lat_tiles[2 * hf + 1]
                emit_h(t0_, l, b, lambda dc: hh[:, dc, 8:520], hb0, tscr, tscrb)
                emit_h(t1_, l, b, lambda dc: hh[:, dc, 520:1032], hb1, tscr, tscrb)
                if hf == 0:
                    emit_h(lat_tiles[2], l, b, lambda dc: hh[:, dc, 0:8], hbA, tscr, tscrb, a=0, n=8)
                    pr = [(lambda dc: hh[:, dc, 8:520], 512, 8, False, [hb0]),
                          (lambda dc: hh[:, dc, 520:1032], 512, 520, False, [hb1]),
                          (lambda dc: hh[:, dc, 0:8], 8, 1032, False, [hbA])]
                    orr = [(lambda dc: hh[:, dc, 8:520], 512, 0, t0_, b, [hb0]),
                           (lambda dc: hh[:, dc, 520:1032], 512, 512, t1_, b, [hb1])]
                    io0 = 8
                else:
                    emit_h(ctx_tile, l, NB, lambda dc: hh[:, dc, 1032:1032 + LC], hb2, tscr, tscrb)
                    pr = [(lambda dc: hhalo[:, dc, :], 8, 8, False, [hhalob]),
                          (lambda dc: hh[:, dc, 8:520], 512, 16, False, [hb0]),
                          (lambda dc: hh[:, dc, 520:1032], 512, 528, False, [hb1]),
                          (lambda dc: hh[:, dc, 1032:1032 + LC], LC, 8, True, [hb2])]
                    orr = [(lambda dc: hh[:, dc, 8:520], 512, 0, t0_, b, [hb0]),
                           (lambda dc: hh[:, dc, 520:1032], 512, 512, t1_, b, [hb1]),
                           (lambda dc: hh[:, dc, 1032:1032 + LC], LC, 1024, ctx_tile, NB, [hb2])]
                    io0 = 16
                for g in range(4):
                    w = POOL_W[g]
                    hw_ = w // 2
                    nsteps = g + 1
                    slot, slb = acquire(base + g)
                    sv = slot[:].rearrange("p (m d k) -> p m d k", m=4, d=8)
                    for m in range(4):
                        k = pk % 2
                        pk += 1
                        for (rf, n, di, isc, rb) in pr:
                            bk = banks.next()
                            K.mm(PSB[bk], PS[bk][:, 0:n], [(sv[:, m, dc, :], rf(dc)) for dc in range(8)],
                                 reads=rb + [slb])
                            dst = PC[k] if isc else PL[k]
                            dstb = PCb[k] if isc else PLb[k]
                            K.op(K.act, lambda h, bk=bk, n=n, di=di, dst=dst: h.copy(
                                out=dst[:, di:di + n], in_=PS[bk][:, 0:n]), reads=[PSB[bk]], writes=[dstb])
                        todo = [(PL[k], PLb[k], T, Tb, 1048, io0, 1024, 0, hf == 0, hf == 1)]
                        if hf == 1:
                            todo.append((PC[k], PCb[k], TC, TCb, 272, 8, LC, 1024, True, True))
                        for (Pb, Pbb, TT, TTb, Ltot, io, no, oc, lfix, rfix) in todo:
                            src, srcb = Pb, Pbb
                            sh = 1
                            lo = 1
                            for stp in range(nsteps):
                                dstT, dstTb = TT[stp % 2], TTb[stp % 2]
                                K.op(K.pool, lambda h, src=src, dstT=dstT, sh=sh, lo=lo, Ltot=Ltot: h.tensor_tensor(
                                    out=dstT[:, lo:Ltot], in0=src[:, lo:Ltot], in1=src[:, lo - sh:Ltot - sh],
                                    op=ALU.add), reads=[srcb], writes=[dstTb])
                                src, srcb = dstT, dstTb
                                sh *= 2
                                lo = 2 * sh - 1
                            so_ = io + hw_ - 1
                            if DBG and hf == 0 and g == 1 and m == 0 and Ltot == 1048 and b == 0:
                                K.dma(K.sp, dbgf[:, 0:1048], Pb[:, :], dbs, reads=[Pbb])
                                K.dma(K.sp, dbgf[:, 1048:2096], src[:, :], dbs, reads=[srcb])
                            K.op(K.dve, lambda h, src=src, Pb=Pb, so_=so_, io=io, no=no, oc=oc, m=m, w=w:
                                 h.scalar_tensor_tensor(out=pooled[:, m, oc:oc + no], in0=src[:, so_:so_ + no],
                                                        scalar=1.0 / w, in1=Pb[:, io:io + no], op0=ALU.mult,
                                                        op1=ALU.subtract), reads=[srcb, Pbb], writes=[poolb[m]])
                            for (fix, side, c_) in ((lfix, 0, 0), (rfix, 1, no - 8)):
                                if not fix:
                                    continue
                                K.op(K.dve, lambda h, src=src, so_=so_, c_=c_, side=side, g=g: h.tensor_tensor(
                                    out=bfix[:], in0=src[:, so_ + c_:so_ + c_ + 8], in1=icnt[:, g, side, :],
                                    op=ALU.mult), reads=[srcb, lb], writes=[bfixb])
                                K.op(K.dve, lambda h, Pb=Pb, io=io, c_=c_, oc=oc, m=m: h.tensor_tensor(
                                    out=pooled[:, m, oc + c_:oc + c_ + 8], in0=bfix[:], in1=Pb[:, io + c_:io + c_ + 8],
                                    op=ALU.subtract), reads=[bfixb, Pbb], writes=[poolb[m]])
                    slotg, slgb = acquire(base + 4 + g)
                    svg = slotg[:].rearrange("p (m d k) -> p m d k", m=4, d=8)
                    slotw, slwb = acquire(base + 8 + g)
                    svw = slotw[:, 0:2048].rearrange("p (e c k) -> p e c k", e=4, c=4)
                    for dd in range(4):
                        for (rf, n, oc, tl, ci, rb) in orr:
                            bk = banks.next()
                            K.mm(PSB[bk], PS[bk][:, 0:n], [(svg[:, dd, dc, :], rf(dc)) for dc in range(8)],
                                 reads=rb + [slgb])
                            K.op(K.act, lambda h, bk=bk, n=n, oc=oc, dd=dd: h.activation(
                                out=sgp[:, dd, oc:oc + n], in_=PS[bk][:, 0:n], func=AF.Silu),
                                reads=[PSB[bk]], writes=[sgpb[dd]])
                    for dd in range(4):
                        for (rf, n, oc, tl, ci, rb) in orr:
                            bk = banks.next()
                            K.mm(PSB[bk], PS[bk][:, 0:n],
                                 [(svw[:, dd, cc, :], pooled[:, cc, oc:oc + n]) for cc in range(4)],
                                 reads=poolb + [slwb])
                            K.op(K.dve, lambda h, bk=bk, n=n, oc=oc, dd=dd, g=g: h.scalar_tensor_tensor(
                                out=y[:, dd, oc:oc + n], in0=PS[bk][:, 0:n], scalar=psc[:, 4 * g + dd:4 * g + dd + 1],
                                in1=sgp[:, dd, oc:oc + n], op0=ALU.mult, op1=ALU.mult),
                                reads=[PSB[bk], sgpb[dd], lb], writes=[yb[dd]])
                    if DBG and hf == 0 and g == 1 and b == 0:
                        K.dma(K.sp, dbgb[:, 0:1024], pooled[:, 0, 0:1024], dbs, reads=poolb)
                        K.dma(K.sp, dbgb[:, 1024:2048], y[:, 3, 0:1024], dbs, reads=yb)
                        K.dma(K.sp, dbgb[:, 2048:3072], sgp[:, 3, 0:1024], dbs, reads=sgpb)
                        K.dma(K.sp, dbgb[:, 3072:4096], hh[:, 0, 8:1032], dbs, reads=[hb0, hb1])
                    sloto, slob = acquire(base + 12 + g)
                    svo = sloto[:].rearrange("p (m e k) -> p m e k", m=8, e=4)
                    for dmc in range(8):
                        for (rf, n, oc, tl, ci, rb) in orr:
                            bk = banks.next()
                            K.mm(PSB[bk], PS[bk][:, 0:n], [(svo[:, dmc, dd, :], y[:, dd, oc:oc + n]) for dd in range(4)],
                                 reads=yb + [slob])
                            emit_resid(tl, l, ci, dmc, bk)
            scope_barrier(allb + ringb)

    def emit_attn(l, b):
        base = BLK_BASE[l]
        with ExitStack() as st:
            allb = []

            def nb_():
                bb = Buf()
                allb.append(bb)
                return bb
            kT = sb(st, "akT", [128, 2, 2, S + LC], BF16)
            kTb = [nb_() for _ in range(5)]
            VZ = sb(st, "aVZ", [128, 18, 2, 192], BF16)
            VZb = [nb_() for _ in range(5)]
            rc_t = sb(st, "arc", [128, 512], F32)
            rs_t = sb(st, "ars", [128, 512], F32)
            ropeb = nb_()
            rps = K.dsem("rps%d" % b)
            E = sb(st, "aE", [128, 8], F32)
            h = sb(st, "ah", [128, 8, 512], BF16)
            hb = nb_()
            tscr = [sb(st, f"ats{k}", [128, 512], F32) for k in range(2)]
            tscrb = [nb_(), nb_()]
            qT = sb(st, "aqT", [128, 8, 512], BF16)
            qTb = [nb_() for _ in range(8)]
            sg = sb(st, "asg", [128, 8, 512], BF16)
            sgb = [nb_() for _ in range(8)]
            y = sb(st, "ay", [128, 8, 512], BF16)
            yb = [nb_() for _ in range(8)]
            xb = [sb(st, f"axb{k}", [128, 512], BF16) for k in range(2)]
            xbb = [nb_(), nb_()]
            ta = [sb(st, f"ata{k}", [128, 512], F32) for k in range(2)]
            tab = [nb_(), nb_()]
            tb2 = [sb(st, f"atb{k}", [128, 512], F32) for k in range(2)]
            tbb = [nb_(), nb_()]
            NPT = 4
            PT = [sb(st, f"aPT{k}", [128, 512], BF16) for k in range(NPT)]
            PTb = [nb_() for _ in range(NPT)]
            dn = sb(st, "adn", [128, 512], F32)
            dnb = nb_()
            of = ta[0]
            ofb = tab[0]
            sd = dn
            sdb = nb_()
            lb = nb_()
            K.dma_group(K.sp, [(E[:], a_sink)], lc_sem(), writes=[lb])
            K.op(K.act, lambda hh_: hh_.activation(out=E[:], in_=E[:], func=AF.Exp), reads=[lb], writes=[lb])
            K.op(K.dve, lambda hh_: hh_.memset(VZ[:], 0.0), writes=VZb)
            K.op(K.pool, lambda hh_: hh_.memset(kT[:], 0.0), writes=kTb)
            tiles = lat_tiles + [ctx_tile]
            emit_rstd(tiles, [lambda dc, n: y[:, dc, 0:n], lambda dc, n: sg[:, dc, 0:n]], sd, sdb)
            rk = [0]

            def load_rope(tl):
                K.dma_group(K.sp, [(rc_t[:], rope_c[:, tl.c0:tl.c0 + 512]), (rs_t[:], rope_s[:, tl.c0:tl.c0 + 512])],
                            rps, writes=[ropeb])

            def rope(bk, n, c0, dst, dstb):
                k = rk[0] % 2
                rk[0] += 1
                actdone = Buf()
                K.op(K.act, lambda hh_: hh_.copy(out=xb[k][:, 0:n], in_=PS[bk][:, 0:n]), reads=[PSB[bk]],
                     writes=[xbb[k], actdone])
                bk2 = banks.next()
                K.mm(PSB[bk2], PS[bk2][:, 0:n], [(Pm, xb[k][:, 0:n])], reads=[xbb[k], CB])
                K.op(K.dve, lambda hh_: hh_.tensor_tensor(out=ta[k][:, 0:n], in0=PS[bk][:, 0:n],
                                                          in1=rc_t[:, 0:n], op=ALU.mult),
                     reads=[PSB[bk], ropeb, actdone], writes=[tab[k]])
                K.op(K.dve, lambda hh_: hh_.tensor_tensor(out=tb2[k][:, 0:n], in0=PS[bk2][:, 0:n],
                                                          in1=rs_t[:, 0:n], op=ALU.mult),
                     reads=[PSB[bk2], ropeb], writes=[tbb[k]])
                if isinstance(dst, list):
                    for (psl, d_) in dst:
                        K.op(K.pool, lambda hh_, psl=psl, d_=d_: hh_.tensor_tensor(
                            out=d_, in0=ta[k][psl, 0:n], in1=tb2[k][psl, 0:n], op=ALU.add),
                            reads=[tab[k], tbb[k]], writes=[dstb])
                else:
                    K.op(K.pool, lambda hh_: hh_.tensor_tensor(out=dst, in0=ta[k][:, 0:n], in1=tb2[k][:, 0:n],
                                                               op=ALU.add),
                         reads=[tab[k], tbb[k]], writes=[dstb])

            for ti, tl in enumerate(tiles):
                n = tl.n
                isc = tl is ctx_tile
                ci = NB if isc else b
                if not isc:
                    load_rope(tl)
                emit_h(tl, l, ci, lambda dc: h[:, dc, 0:n], hb, tscr, tscrb)
                slot, slb = acquire(base + 2)
                sv = slot[:].rearrange("p (m d k) -> p m d k", m=4, d=8)
                kc0 = tl.rs0
                for i in range(2):
                    bk = banks.next()
                    K.mm(PSB[bk], PS[bk][:, 0:n], [(sv[:, i, dc, :], h[:, dc, 0:n]) for dc in range(8)], reads=[hb, slb])
                    if isc:
                        K.op(K.act, lambda hh_, bk=bk, i=i: hh_.copy(out=kT[0:64, i, 0, kc0:kc0 + n],
                                                                     in_=PS[bk][0:64, 0:n]),
                             reads=[PSB[bk]], writes=[kTb[ti]])
                        K.op(K.act, lambda hh_, bk=bk, i=i: hh_.copy(out=kT[64:128, i, 1, kc0:kc0 + n],
                                                                     in_=PS[bk][64:128, 0:n]),
                             reads=[PSB[bk]], writes=[kTb[ti]])
                    else:
                        rope(bk, n, tl.c0, [(slice(0, 64), kT[0:64, i, 0, kc0:kc0 + n]),
                                            (slice(64, 128), kT[64:128, i, 1, kc0:kc0 + n])], kTb[ti])
                for jj in range(n // 128):
                    ch = kc0 // 128 + jj
                    bk = banks.next()
                    K.mm(PSB[bk], PS[bk][:, 0:256],
                         [(h[:, dc, jj * 128:(jj + 1) * 128], sv[:, 2:4, dc, :]) for dc in range(8)], reads=[hb, slb])
                    pv = PS[bk][:, 0:256].rearrange("p (i c) -> p i c", i=2)
                    K.op(K.act, lambda hh_, ch=ch, pv=pv: hh_.copy(out=VZ[:, ch, :, 0:64], in_=pv[:, :, 0:64]),
                         reads=[PSB[bk]], writes=[VZb[ti]])
                    K.op(K.act, lambda hh_, ch=ch, pv=pv: hh_.copy(out=VZ[:, ch, :, 128:192], in_=pv[:, :, 64:128]),
                         reads=[PSB[bk]], writes=[VZb[ti]])
            banks.set([4, 5, 6, 7])
            ASTG = int(_os.environ.get("ASTG", "9"))
            def q_proj(tq):
                for blk in range(2):
                    slot, slb = acquire(base + blk)
                    sv = slot[:].rearrange("p (m d k) -> p m d k", m=4, d=8)
                    for m in range(4):
                        qc = 4 * blk + m
                        bk = banks.next()
                        K.mm(PSB[bk], PS[bk][:, :], [(sv[:, m, dc, :], h[:, dc, :]) for dc in range(8)], reads=[hb, slb])
                        rope(bk, 512, tq.c0, qT[:, qc, :], qTb[qc])

            def g_proj():
                for blk in range(2):
                    slot, slb = acquire(base + 3 + blk)
                    sv = slot[:].rearrange("p (m d k) -> p m d k", m=4, d=8)
                    for m in range(4):
                        qc = 4 * blk + m
                        bk = banks.next()
                        K.mm(PSB[bk], PS[bk][:, :], [(sv[:, m, dc, :], h[:, dc, :]) for dc in range(8)], reads=[hb, slb])
                        K.op(K.act, lambda hh_, bk=bk, qc=qc: hh_.activation(out=sg[:, qc, :], in_=PS[bk][:, :],
                                                                             func=AF.Silu),
                             reads=[PSB[bk]], writes=[sgb[qc]])

            load_rope(lat_tiles[0])
            emit_h(lat_tiles[0], l, b, lambda dc: h[:, dc, :], hb, tscr, tscrb)
            q_proj(lat_tiles[0])
            g_proj()
            for ti, tl in enumerate(lat_tiles):
                n = 512
                if ti + 1 < 4:
                    load_rope(lat_tiles[ti + 1])
                    emit_h(lat_tiles[ti + 1], l, b, lambda dc: h[:, dc, :], hb, tscr, tscrb)
                steps = []
                for nl in range(4):
                    nblk = 4 * ti + nl
                    for i in range(2):
                        chunks = [(nblk, None), (16, None), (17, None)]
                        if nblk > 0:
                            chunks.append((nblk - 1, maskP))
                        if nblk < 15:
                            chunks.append((nblk + 1, maskN))
                        tot = 2 * len(chunks)
                        q_ = 0
                        for half in range(2):
                            for (kc, msk) in chunks:
                                steps.append((nl, i, half, kc, msk, q_ == 0, q_ == tot - 1))
                                q_ += 1
                DEP = 3
                stb = {}

                def emit_qk(si):
                    (nl, i, half, kc, msk, first, last) = steps[si]
                    ps_ = slice(0, 64) if half == 0 else slice(64, 128)
                    bk = banks.next()
                    kti = min(kc // 4, 4)
                    K.op(K.pe, lambda hh_: hh_.matmul(
                        PS[bk][:, :], kT[:, i, half, kc * 128:(kc + 1) * 128], qT[:, 4 * i:4 * i + 4, nl * 128:(nl + 1) * 128],
                        start=True, stop=True), reads=[kTb[kti]] + qTb[4 * i:4 * i + 4], writes=[PSB[bk]])
                    p = si % NPT
                    K.op(K.act, lambda hh_: hh_.activation(out=PT[p][:], in_=PS[bk][:, :], func=AF.Exp, scale=0.125),
                         reads=[PSB[bk]], writes=[PTb[p]])
                    if msk is not None:
                        K.op(K.pool, lambda hh_: hh_.tensor_tensor(out=PT[p][:], in0=PT[p][:], in1=msk, op=ALU.mult),
                             reads=[PTb[p], CB], writes=[PTb[p]])

                def emit_pv(si):
                    (nl, i, half, kc, msk, first, last) = steps[si]
                    grp = (nl * 2 + i) % 2
                    nb_k, db_k = 2 * grp, 2 * grp + 1
                    p = si % NPT
                    vs = slice(0, 128) if half == 0 else slice(64, 192)
                    kti = min(kc // 4, 4)
                    K.op(K.pe, lambda hh_: hh_.matmul(PS[nb_k][:, :], VZ[:, kc, i, vs], PT[p][:], start=first, stop=last),
                         reads=[PTb[p], VZb[kti]], writes=[PSB[nb_k]], signal=False)
                    K.op(K.pe, lambda hh_: hh_.matmul(PS[db_k][:, :], OZ[:, vs], PT[p][:], start=first, stop=last),
                         reads=[PTb[p], CB], writes=[PSB[db_k]], signal=True)
                    if last:
                        K.op(K.dve, lambda hh_: hh_.tensor_tensor(
                            out=dn[:].rearrange("p (a q) -> p a q", a=4),
                            in0=PS[db_k][:, :].rearrange("p (a q) -> p a q", a=4),
                            in1=E[:, 4 * i:4 * i + 4].unsqueeze(2).to_broadcast([128, 4, 128]), op=ALU.add),
                            reads=[PSB[db_k], lb], writes=[dnb])
                        K.op(K.dve, lambda hh_: hh_.reciprocal(out=dn[:], in_=dn[:]), reads=[dnb], writes=[dnb])
                        K.op(K.dve, lambda hh_: hh_.tensor_tensor(out=of[:], in0=PS[nb_k][:, :], in1=dn[:], op=ALU.mult),
                             reads=[PSB[nb_k], dnb], writes=[ofb])
                        K.op(K.dve, lambda hh_: hh_.tensor_tensor(
                            out=y[:, 4 * i:4 * i + 4, nl * 128:(nl + 1) * 128],
                            in0=of[:].rearrange("p (a q) -> p a q", a=4),
                            in1=sg[:, 4 * i:4 * i + 4, nl * 128:(nl + 1) * 128], op=ALU.mult),
                            reads=[ofb] + sgb[4 * i:4 * i + 4], writes=yb[4 * i:4 * i + 4])

                ns = len(steps)
                if ASTG < 3:
                    ns = 0

                def qk_deps(si):
                    (nl, i, half, kc, msk, first, last) = steps[si]
                    bk = banks.ids[banks.i % len(banks.ids)]
                    return [kTb[min(kc // 4, 4)]] + qTb[4 * i:4 * i + 4], [PSB[bk]]

                def pv_deps(si):
                    (nl, i, half, kc, msk, first, last) = steps[si]
                    grp = (nl * 2 + i) % 2
                    return [PTb[si % NPT], VZb[min(kc // 4, 4)], CB], [PSB[2 * grp], PSB[2 * grp + 1]]

                for si in range(min(DEP, ns)):
                    emit_qk(si)
                for si in range(ns):
                    if si + DEP < ns:
                        r_, w_ = qk_deps(si + DEP)
                        K._deps(K.pe, r_, w_)
                    r_, w_ = pv_deps(si)
                    K._deps(K.pe, r_, w_)
                    if si + DEP < ns:
                        emit_qk(si + DEP)
                    emit_pv(si)
                if ti + 1 < 4:
                    q_proj(lat_tiles[ti + 1])
                for blk in range(2):
                    slot, slb = acquire(base + 5 + blk)
                    so = slot[:].rearrange("p (m c k) -> p m c k", m=4, c=8)
                    for m in range(4):
                        dmc = 4 * blk + m
                        bk = banks.next()
                        K.mm(PSB[bk], PS[bk][:, :], [(so[:, m, qc, :], y[:, qc, :]) for qc in range(8)],
                             reads=yb + [slb])
                        emit_resid(tl, l, b, dmc, bk)
                if ti + 1 < 4:
                    g_proj()
            banks.set(range(8))
            scope_barrier(allb + ringb + PSB)

    def emit_final(b):
        with ExitStack() as st:
            allb = []
            sq = sb(st, "fsq", [128, 16, 512], BF16)
            sd = sb(st, "fsd", [128, 512], F32)
            sdb = Buf()
            emit_rstd(lat_tiles, [lambda dc, n: sq[:, dc, 0:n], lambda dc, n: sq[:, 8 + dc, 0:n]], sd, sdb)
            for ti, tl in enumerate(lat_tiles):
                for dc in range(8):
                    K.op(K.dve, lambda h, tl=tl, dc=dc: h.scalar_tensor_tensor(
                        out=tl.x(dc), in0=tl.x(dc), scalar=fg_t[:, dc:dc + 1], in1=rstd[:, tl.rs0:tl.rs0 + 512],
                        op0=ALU.mult, op1=ALU.mult), reads=[RB[ti], tl.rbuf, CB], writes=[tl.rbuf])
                K.dma(K.sp, outT[b][:, :, 512 * ti:512 * ti + 512].rearrange("c p t -> p c t"),
                      X[:, :, 512 * ti:512 * ti + 512], os_t[ti], reads=[XB[ti]])
            scope_barrier(allb)

    plan_weights()
    for b in range(NB):
        if b > 0:
            load_inputs(b)
        for li, l in enumerate(layers):
            if b == 0 and li + 1 < len(layers):
                convert(layers[li + 1])
            if l == 0:
                emit_gmlp(0, 0, b, lat_tiles + [ctx_tile])
            elif l == 1:
                emit_pool(1, b)
            elif l == 2:
                emit_attn(2, b)
            else:
                emit_gmlp(3, 1, b, lat_tiles)
        emit_final(b)
    for t in range(4):
        K.sp.h.wait_ge(os_t[t].h, os_t[t].cnt)
    assert wstate["used"] == len(wseq), (wstate, len(wseq))
    for e in (K.pe, K.act, K.dve, K.pool):
        assert e.cnt < 60000, (e.name, e.cnt)
    es.close()
    return nc, K


def _qperm():
    cols = []
    for i in range(2):
        for a in range(4):
            hlo = 4 * (2 * i) + a
            hhi = 4 * (2 * i + 1) + a
            cols += list(range(hlo * 64, hlo * 64 + 64)) + list(range(hhi * 64, hhi * 64 + 64))
    return np.array(cols)


def _win_blocks(W):
    ncc = W.shape[1] // 128
    t = W.reshape(8, 128, ncc, 128).transpose(2, 1, 0, 3)
    t = t.reshape(ncc // 4, 4, 128, 8, 128).transpose(0, 2, 1, 3, 4)
    return np.ascontiguousarray(t).reshape(ncc // 4, 128, 4096)


def _rope_tables():
    import jax
    import jax.numpy as jnp
    with jax.default_device(jax.devices("cpu")[0]):
        rows = S // 64
        row = jnp.repeat(jnp.arange(rows), 64).astype(jnp.float32)
        col = jnp.tile(jnp.arange(64), rows).astype(jnp.float32)
        nf = 16
        inv = 10000.0 ** (-jnp.arange(nf, dtype=jnp.float32) / nf)
        ang = jnp.concatenate([row[:, None] * inv, col[:, None] * inv], axis=-1)
        cos = np.asarray(jnp.cos(ang)).T.astype(np.float32)
        sin = np.asarray(jnp.sin(ang)).T.astype(np.float32)
    c = np.tile(cos, (4, 1))
    s = np.concatenate([-sin, sin, -sin, sin], axis=0)
    return np.ascontiguousarray(c), np.ascontiguousarray(s)


def _shared_layout(inp):
    f = lambda a: np.ascontiguousarray(np.asarray(a, dtype=np.float32))
    d = {}
    d["adaw"] = f(inp["ada_w"].reshape(4, 8, 128, 24, 128).transpose(0, 3, 2, 1, 4))
    d["adab"] = f(inp["ada_b"].reshape(4, 24, 128).transpose(2, 0, 1))
    d["ng"] = f(inp["norm_g"].reshape(4, 8, 128).transpose(2, 0, 1))
    d["fg"] = f(inp["final_g"].reshape(8, 128).T)
    blocks = []
    qp = _qperm()
    for j, l in ((0, 0),):
        pass
    def gm(j):
        out = [_win_blocks(inp["gmlp_w_in"][j])]
        W = inp["gmlp_w_out"][j]
        t = W.reshape(16, 128, 8, 128).transpose(2, 1, 0, 3)
        t = t.reshape(4, 2, 128, 16, 128).transpose(0, 2, 1, 3, 4)
        out.append(np.ascontiguousarray(t).reshape(4, 128, 4096))
        return out
    blocks += gm(0)
    blocks.append(_win_blocks(inp["pool_w_in"][0]))
    wg = inp["pool_w_grp"][0]
    t = wg.reshape(4, 4, 128, 4, 128).transpose(0, 2, 3, 1, 4)
    t = np.ascontiguousarray(t).reshape(4, 128, 2048)
    blocks.append(np.concatenate([t, np.zeros_like(t)], axis=2))
    W = inp["pool_w_out"][0]
    t = W.reshape(4, 4, 128, 8, 128).transpose(0, 2, 3, 1, 4)
    blocks.append(np.ascontiguousarray(t).reshape(4, 128, 4096))
    W = inp["attn_w_in"][0]
    order = np.concatenate([qp, np.arange(1024, 1536), 1536 + qp])
    blocks.append(_win_blocks(np.ascontiguousarray(W[:, order])))
    W = inp["attn_w_out"][0][qp, :]
    t = W.reshape(8, 128, 8, 128).transpose(2, 1, 0, 3)
    t = t.reshape(2, 4, 128, 8, 128).transpose(0, 2, 1, 3, 4)
    blocks.append(np.ascontiguousarray(t).reshape(2, 128, 4096))
    blocks += gm(1)
    d["wts"] = f(np.concatenate(blocks, axis=0))
    assert d["wts"].shape == (NBLK, 128, 4096)
    d["g_wsT"] = f(inp["gmlp_w_s"].transpose(0, 3, 1, 2))
    d["g_bsbc"] = f(np.broadcast_to(inp["gmlp_b_s"][:, None, :, :], (2, 128, 8, 128)))
    d["g_vng"] = f(inp["gmlp_vnorm_g"].reshape(2, 16, 128).transpose(0, 2, 1))
    d["g_vnb"] = f(inp["gmlp_vnorm_b"].reshape(2, 16, 128).transpose(0, 2, 1))
    d["p_scale"] = f(inp["pool_scale"][0].reshape(16, 128).T)
    ic = np.zeros((128, 4, 2, 8), np.float32)
    for g, w in enumerate(POOL_W):
        for t_ in range(8):
            ic[:, g, 0, t_] = np.float32(1.0) / np.float32(min(t_ + w // 2, S) - max(t_ - w // 2, 0))
            tt = S - 8 + t_
            ic[:, g, 1, t_] = np.float32(1.0) / np.float32(min(tt + w // 2, S) - max(tt - w // 2, 0))
    d["p_icnt"] = ic
    sk = inp["attn_sink"][0]
    e = np.zeros((128, 8), np.float32)
    for i in range(2):
        for a in range(4):
            e[0:64, 4 * i + a] = sk[4 * (2 * i) + a]
            e[64:128, 4 * i + a] = sk[4 * (2 * i + 1) + a]
    d["a_sink"] = e
    d["rope_c"], d["rope_s"] = _rope_tables()
    cst = np.zeros((128, 1152), np.float32)
    for m in range(128):
        partner = m + 32 if (m % 64) < 32 else m - 32
        cst[partner, m] = 1.0
    jj = np.arange(128)[:, None]
    ii = np.arange(128)[None, :]
    cst[:, 128:640] = np.tile((ii <= jj).astype(np.float32), (1, 4))
    cst[:, 640:1152] = np.tile((jj <= ii).astype(np.float32), (1, 4))
    d["cstf"] = cst
    return d


def _core_layout(inp, b0, NB):
    f = lambda a: np.ascontiguousarray(np.asarray(a, dtype=np.float32))
    d = {}
    d["xT"] = f(inp["x"][b0:b0 + NB].transpose(0, 2, 1).reshape(NB, 8, 128, S))
    d["cT"] = f(inp["ctx"][b0:b0 + NB].transpose(0, 2, 1).reshape(NB, 8, 128, LC))
    cc = np.concatenate([inp["c"][b0:b0 + NB], inp["c_ctx"][None, :]], axis=0)
    d["condT"] = f(cc.reshape(NB + 1, 8, 128).transpose(2, 1, 0))
    return d


_PROG = {}


def kernel(**inputs):
    inp = {k: np.asarray(v) for k, v in inputs.items()}
    NB = 4
    if "p" not in _PROG:
        _PROG["p"] = build_program(NB)[0]
    nc = _PROG["p"]
    shared = _shared_layout(inp)
    in_maps = []
    for c in range(8):
        m = dict(shared)
        m.update(_core_layout(inp, c * NB, NB))
        in_maps.append(m)
    res = run_bass_kernel_spmd(nc, in_maps, core_ids=list(range(8)))
    outs = []
    for c in range(8):
        o = np.asarray(res.results[c]["outT"]).reshape(NB, D, S).transpose(0, 2, 1)
        outs.append(o)
    return np.ascontiguousarray(np.concatenate(outs, axis=0)).astype(np.float32)
```
